# Optimizing a Trainium2 kernel written in Bass

```python
import jax, jax.numpy as jnp
from jax import lax
import numpy as np

D_MODEL = 1024
BATCH = 4
SEQ = 8192
DEPTH = 4

HEAD_DIM = 64
A_HEADS = 8
A_KV_HEADS = 2
B_HEADS = 8
B_KV_HEADS = 2
WINDOW = 128
BLOCK = 128
ROPE_THETA = 10000.0
GRID_W = 64
SGU_WIDTH = D_MODEL
SGU_GROUPS = 8
SGU_CHUNK = 128
D_FF = 4 * D_MODEL
EPS = 1e-6
N_ATT_LAYERS = (DEPTH + 1) // 2
N_SGU_LAYERS = DEPTH // 2

A_Q = A_HEADS * HEAD_DIM
A_KV = A_KV_HEADS * HEAD_DIM
B_Q = B_HEADS * HEAD_DIM
B_KV = B_KV_HEADS * HEAD_DIM
ATT_IN = A_Q + 2 * A_KV + B_Q + 2 * B_KV
ATT_OUT_IN = A_Q + B_Q

kernel_name = "hybrid_window_grid_attn_sgu_encoder"


def _rmsnorm(x, g):
    xf = x.astype(jnp.float32)
    y = xf * lax.rsqrt(jnp.mean(xf * xf, axis=-1, keepdims=True) + EPS)
    return (y * g.astype(jnp.float32)).astype(x.dtype)


def _layernorm(x, g, b):
    xf = x.astype(jnp.float32)
    mu = jnp.mean(xf, axis=-1, keepdims=True)
    var = jnp.mean(jnp.square(xf - mu), axis=-1, keepdims=True)
    y = (xf - mu) * lax.rsqrt(var + EPS)
    return (y * g.astype(jnp.float32) + b.astype(jnp.float32)).astype(x.dtype)


def _rope_angles(pos, dim):
    freqs = ROPE_THETA ** (-jnp.arange(0, dim, 2, dtype=jnp.float32) / dim)
    ang = pos.astype(jnp.float32)[:, None] * freqs[None, :]
    return jnp.cos(ang), jnp.sin(ang)


def _apply_rope(x, cos, sin):
    xf = x.astype(jnp.float32)
    half = xf.shape[-1] // 2
    x1, x2 = xf[..., :half], xf[..., half:]
    c, s = cos[None, :, None, :], sin[None, :, None, :]
    return jnp.concatenate([x1 * c - x2 * s, x2 * c + x1 * s], axis=-1).astype(x.dtype)


def _apply_axial_rope(x, cos_r, sin_r, cos_c, sin_c):
    half = x.shape[-1] // 2
    return jnp.concatenate([_apply_rope(x[..., :half], cos_r, sin_r),
                            _apply_rope(x[..., half:], cos_c, sin_c)], axis=-1)


def _window_attention(q, k, v, sink):
    bsz, s_len = q.shape[0], q.shape[1]
    nb = s_len // BLOCK
    g = A_HEADS // A_KV_HEADS
    scale = HEAD_DIM ** -0.5
    qb = q.reshape(bsz, nb, BLOCK, A_KV_HEADS, g, HEAD_DIM).astype(jnp.float32)
    pad = ((0, 0), (BLOCK, BLOCK), (0, 0), (0, 0))
    kp = jnp.pad(k, pad).reshape(bsz, nb + 2, BLOCK, A_KV_HEADS, HEAD_DIM)
    vp = jnp.pad(v, pad).reshape(bsz, nb + 2, BLOCK, A_KV_HEADS, HEAD_DIM)
    kband = jnp.concatenate([kp[:, :-2], kp[:, 1:-1], kp[:, 2:]], axis=2).astype(jnp.float32)
    vband = jnp.concatenate([vp[:, :-2], vp[:, 1:-1], vp[:, 2:]], axis=2).astype(jnp.float32)
    s = jnp.einsum('bnqhgd,bnjhd->bnhgqj', qb, kband) * scale
    qi = jnp.arange(BLOCK)
    kj = jnp.arange(3 * BLOCK)
    rel = kj[None, :] - BLOCK - qi[:, None]
    kpos = jnp.arange(nb)[:, None] * BLOCK - BLOCK + kj[None, :]
    mask = (jnp.abs(rel) <= WINDOW)[None, :, :] & ((kpos >= 0) & (kpos < s_len))[:, None, :]
    s = jnp.where(mask[None, :, None, None, :, :], s, -1e30)
    sink_b = sink.astype(jnp.float32).reshape(A_KV_HEADS, g)[None, None, :, :, None, None]
    m = jnp.maximum(jnp.max(s, axis=-1, keepdims=True), sink_b)
    p = jnp.exp(s - m)
    denom = jnp.sum(p, axis=-1, keepdims=True) + jnp.exp(sink_b - m)
    o = jnp.einsum('bnhgqj,bnjhd->bnqhgd', p / denom, vband)
    return o.reshape(bsz, s_len, A_Q).astype(q.dtype)


def _grid_attention(q, k, v):
    bsz, s_len = q.shape[0], q.shape[1]
    nb = s_len // BLOCK
    g = B_HEADS // B_KV_HEADS
    scale = HEAD_DIM ** -0.5
    qb = q.reshape(bsz, nb, BLOCK, B_KV_HEADS, g, HEAD_DIM).transpose(1, 0, 2, 3, 4, 5)
    kf = k.astype(jnp.float32)
    vf = v.astype(jnp.float32)

    def one_block(qblk):
        s = jnp.einsum('bqhgd,bkhd->bhgqk', qblk.astype(jnp.float32), kf) * scale
        p = jax.nn.softmax(s, axis=-1)
        return jnp.einsum('bhgqk,bkhd->bqhgd', p, vf)

    o = lax.map(one_block, qb)
    return o.transpose(1, 0, 2, 3, 4, 5).reshape(bsz, s_len, B_Q).astype(q.dtype)


def _attention_layer(x, norm_g, w_in, sink, qn_g, kn_g, w_out,
                     cos1, sin1, cos_r, sin_r, cos_c, sin_c):
    bsz, s_len = x.shape[0], x.shape[1]
    h = _rmsnorm(x, norm_g)
    proj = h @ w_in
    offs = [A_Q, A_Q + A_KV, A_Q + 2 * A_KV, A_Q + 2 * A_KV + B_Q, A_Q + 2 * A_KV + B_Q + B_KV]
    qa, ka, va, qb, kb, vb = jnp.split(proj, offs, axis=-1)
    qa = qa.reshape(bsz, s_len, A_HEADS, HEAD_DIM)
    ka = ka.reshape(bsz, s_len, A_KV_HEADS, HEAD_DIM)
    va = va.reshape(bsz, s_len, A_KV_HEADS, HEAD_DIM)
    qb = qb.reshape(bsz, s_len, B_HEADS, HEAD_DIM)
    kb = kb.reshape(bsz, s_len, B_KV_HEADS, HEAD_DIM)
    vb = vb.reshape(bsz, s_len, B_KV_HEADS, HEAD_DIM)
    qa = _apply_rope(qa, cos1, sin1)
    ka = _apply_rope(ka, cos1, sin1)
    oa = _window_attention(qa, ka, va, sink)
    qb = _apply_axial_rope(_rmsnorm(qb, qn_g), cos_r, sin_r, cos_c, sin_c)
    kb = _apply_axial_rope(_rmsnorm(kb, kn_g), cos_r, sin_r, cos_c, sin_c)
    ob = _grid_attention(qb, kb, vb)
    return x + jnp.concatenate([oa, ob], axis=-1) @ w_out


def _sgu_layer(x, norm_g, w_in, ln_g, ln_b, w_s, b_s, w_out):
    bsz, s_len = x.shape[0], x.shape[1]
    nc = s_len // SGU_CHUNK
    dg = SGU_WIDTH // SGU_GROUPS
    h = _rmsnorm(x, norm_g)
    z = jax.nn.gelu(h @ w_in)
    u, v = jnp.split(z, 2, axis=-1)
    v = _layernorm(v, ln_g, ln_b)
    vb = v.reshape(bsz, nc, SGU_CHUNK, SGU_GROUPS, dg)
    mixed = jnp.einsum('gpq,bnqgd->bnpgd', w_s, vb) + b_s.T[None, None, :, :, None]
    y = u * mixed.reshape(bsz, s_len, SGU_WIDTH)
    return x + y @ w_out


def _mlp(x, norm_g, w1, w2):
    h = _rmsnorm(x, norm_g)
    return x + jnp.square(jax.nn.relu(h @ w1)) @ w2


def setup_inputs(seed: int = 0) -> dict:
    key = jax.random.key(seed)
    ks = jax.random.split(key, 20)
    f32 = jnp.float32
    nrm = lambda k, shape, s: jax.random.normal(k, shape, f32) * s
    return {
        "x": jax.random.normal(ks[0], (BATCH, SEQ, D_MODEL), f32),
        "att_norm": 1.0 + nrm(ks[1], (N_ATT_LAYERS, D_MODEL), 0.02),
        "att_w_in": nrm(ks[2], (N_ATT_LAYERS, D_MODEL, ATT_IN), D_MODEL ** -0.5),
        "att_sink": nrm(ks[3], (N_ATT_LAYERS, A_HEADS), 0.5),
        "att_qnorm": 1.0 + nrm(ks[4], (N_ATT_LAYERS, HEAD_DIM), 0.02),
        "att_knorm": 1.0 + nrm(ks[5], (N_ATT_LAYERS, HEAD_DIM), 0.02),
        "att_w_out": nrm(ks[6], (N_ATT_LAYERS, ATT_OUT_IN, D_MODEL), ATT_OUT_IN ** -0.5),
        "sgu_norm": 1.0 + nrm(ks[7], (N_SGU_LAYERS, D_MODEL), 0.02),
        "sgu_w_in": nrm(ks[8], (N_SGU_LAYERS, D_MODEL, 2 * SGU_WIDTH), D_MODEL ** -0.5),
        "sgu_ln_g": 1.0 + nrm(ks[9], (N_SGU_LAYERS, SGU_WIDTH), 0.02),
        "sgu_ln_b": nrm(ks[10], (N_SGU_LAYERS, SGU_WIDTH), 0.02),
        "sgu_w_s": nrm(ks[11], (N_SGU_LAYERS, SGU_GROUPS, SGU_CHUNK, SGU_CHUNK), SGU_CHUNK ** -0.5),
        "sgu_b_s": 1.0 + nrm(ks[12], (N_SGU_LAYERS, SGU_GROUPS, SGU_CHUNK), 0.1),
        "sgu_w_out": nrm(ks[13], (N_SGU_LAYERS, SGU_WIDTH, D_MODEL), SGU_WIDTH ** -0.5),
        "mlp_norm": 1.0 + nrm(ks[14], (DEPTH, D_MODEL), 0.02),
        "mlp_w1": nrm(ks[15], (DEPTH, D_MODEL, D_FF), D_MODEL ** -0.5),
        "mlp_w2": nrm(ks[16], (DEPTH, D_FF, D_MODEL), D_FF ** -0.5),
        "final_norm": 1.0 + nrm(ks[17], (D_MODEL,), 0.02),
    }


def reference(x, att_norm, att_w_in, att_sink, att_qnorm, att_knorm, att_w_out,
              sgu_norm, sgu_w_in, sgu_ln_g, sgu_ln_b, sgu_w_s, sgu_b_s, sgu_w_out,
              mlp_norm, mlp_w1, mlp_w2, final_norm):
    s_len = x.shape[1]
    pos = jnp.arange(s_len)
    rows = s_len // GRID_W
    row_idx = jnp.repeat(jnp.arange(rows), GRID_W)
    col_idx = jnp.tile(jnp.arange(GRID_W), rows)
    cos1, sin1 = _rope_angles(pos, HEAD_DIM)
    cos_r, sin_r = _rope_angles(row_idx, HEAD_DIM // 2)
    cos_c, sin_c = _rope_angles(col_idx, HEAD_DIM // 2)
    h = x
    for layer in range(DEPTH):
        i = layer // 2
        if layer % 2 == 0:
            h = _attention_layer(h, att_norm[i], att_w_in[i], att_sink[i], att_qnorm[i],
                                 att_knorm[i], att_w_out[i],
                                 cos1, sin1, cos_r, sin_r, cos_c, sin_c)
        else:
            h = _sgu_layer(h, sgu_norm[i], sgu_w_in[i], sgu_ln_g[i], sgu_ln_b[i],
                           sgu_w_s[i], sgu_b_s[i], sgu_w_out[i])
        h = _mlp(h, mlp_norm[layer], mlp_w1[layer], mlp_w2[layer])
    return _rmsnorm(h, final_norm)
```

```python
import numpy as np
import ml_dtypes
import concourse.bass as bass
import concourse.mybir as mybir
from concourse.bass_utils import run_bass_kernel_spmd

F32 = mybir.dt.float32
BF16 = mybir.dt.bfloat16
AF = mybir.ActivationFunctionType
ALU = mybir.AluOpType
AX = mybir.AxisListType

NCORES = 8
T = 4096
NT = 32
NG = 8
D = 1024
DFF = 4096
EPS = 1e-6
ATT_IN = 1536

ENGS = ("pe", "act", "dve", "pool", "sp")
GRID_SHIFT = -8.0


class Op:
    __slots__ = ("eng", "fn", "deps", "is_dma", "dma_key", "dma_val", "sem_val", "needs_inc", "idx", "dma_inc")

    def __init__(self, eng, fn, is_dma, dma_key):
        self.eng = eng
        self.fn = fn
        self.deps = []
        self.is_dma = is_dma
        self.dma_key = dma_key
        self.dma_val = 0
        self.sem_val = 0
        self.needs_inc = False
        self.idx = 0
        self.dma_inc = 16


class Sched:
    def __init__(self, nc, same_engine_sync=True):
        self.nc = nc
        self.ops = {e: [] for e in ENGS}
        self.last_w = {}
        self.readers = {}
        self.dma_last = {}
        self.dma_cnt = {}
        self.same_engine_sync = same_engine_sync
        self.all_ops = []
        self.barrier_deps = []
        self.barrier_seen = set(ENGS)

    def barrier(self):
        deps = []
        for e in ENGS:
            if self.ops[e]:
                deps.append(self.ops[e][-1])
        for k, op in self.dma_last.items():
            deps.append(op)
        self.barrier_deps = deps
        self.barrier_seen = set()
        self.last_w = {}
        self.readers = {}

    def add(self, eng, fn, reads=(), writes=(), dma_key=None, dma_inc=16):
        is_dma = dma_key is not None
        op = Op(eng, fn, is_dma, dma_key)
        op.dma_inc = dma_inc
        deps = []
        if eng not in self.barrier_seen:
            deps.extend(self.barrier_deps)
            self.barrier_seen.add(eng)
        for t in reads:
            w = self.last_w.get(t)
            if w is not None:
                deps.append(w)
        for t in writes:
            w = self.last_w.get(t)
            if w is not None:
                deps.append(w)
            deps.extend(self.readers.get(t, ()))
        if is_dma:
            p = self.dma_last.get(dma_key)
            if p is not None:
                deps.append(p)
            self.dma_last[dma_key] = op
            self.dma_cnt[dma_key] = self.dma_cnt.get(dma_key, 0) + dma_inc
            op.dma_val = self.dma_cnt[dma_key]
        for t in reads:
            self.readers.setdefault(t, []).append(op)
        for t in writes:
            self.last_w[t] = op
            self.readers[t] = []
        best = {}
        for d in deps:
            if d is op:
                continue
            if d.is_dma:
                k = ("d", d.dma_key)
                v = d.dma_val
            else:
                k = ("e", d.eng)
                v = d.idx
            cur = best.get(k)
            if cur is None or v > cur[0]:
                best[k] = (v, d)
        op.deps = [d for _, d in best.values()]
        op.idx = len(self.ops[eng])
        self.ops[eng].append(op)
        self.all_ops.append(op)
        return op

    def dma(self, eng, out, in_, reads, writes, key, slow=False):
        if slow:
            return self.add(eng, lambda e, o=out, i=in_: e.dma_start(out=o, in_=i, allow_slow_non_contiguous=True),
                            reads, writes, dma_key=key)
        return self.add(eng, lambda e, o=out, i=in_: e.dma_start(out=o, in_=i), reads, writes, dma_key=key)

    def _skip_same(self, d, ename):
        if d.eng != ename:
            return False
        if not self.same_engine_sync:
            return True
        return ename == "pe"

    def emit(self):
        nc = self.nc
        for op in self.all_ops:
            for d in op.deps:
                if d.is_dma or self._skip_same(d, op.eng):
                    continue
                d.needs_inc = True
        for e in ENGS:
            c = 0
            for op in self.ops[e]:
                if op.needs_inc and not op.is_dma:
                    c += 1
                    op.sem_val = c
        eng_sem = {e: nc.alloc_semaphore("prog_" + e) for e in ENGS}
        dma_sem = {k: nc.alloc_semaphore("dq_%d" % i) for i, k in enumerate(self.dma_cnt)}

        def emit_engine(ename, e):
            waited = {}
            for op in self.ops[ename]:
                need = {}
                for d in op.deps:
                    if d.is_dma:
                        s = ("d", d.dma_key)
                        v = d.dma_val
                    else:
                        if self._skip_same(d, ename):
                            continue
                        s = ("e", d.eng)
                        v = d.sem_val
                    if v > need.get(s, 0):
                        need[s] = v
                for s, v in need.items():
                    if waited.get(s, 0) >= v:
                        continue
                    waited[s] = v
                    e.wait_ge(dma_sem[s[1]] if s[0] == "d" else eng_sem[s[1]], v)
                ins = op.fn(e)
                if op.is_dma:
                    ins.then_inc(dma_sem[op.dma_key], op.dma_inc)
                elif op.needs_inc:
                    ins.then_inc(eng_sem[ename], 1)
            if ename == "sp":
                for k, h in dma_sem.items():
                    e.wait_ge(h, self.dma_cnt[k])

        with nc.Block() as block:
            @block.tensor
            def _(e):
                emit_engine("pe", e)

            @block.scalar
            def _(e):
                emit_engine("act", e)

            @block.vector
            def _(e):
                emit_engine("dve", e)

            @block.gpsimd
            def _(e):
                emit_engine("pool", e)

            @block.sync
            def _(e):
                emit_engine("sp", e)


DT_BYTES = {F32: 4, BF16: 2}


class Builder:
    def __init__(self, phases, fused):
        self.nc = nc = bass.Bass("TRN2", target_bir_lowering=False)
        self.S = Sched(nc)
        self.phases = phases
        self.fused = fused
        self.uid = 0
        self.wkey = 0
        self._ein = {}
        self.x_in = self.inp("x")
        self.out = nc.dram_tensor("out", [T, D], F32, kind="ExternalOutput").ap()
        it = lambda n, s, d=F32: nc.dram_tensor(n, s, d).ap()
        self.X = it("Xs", [T, D])
        self.XM = it("XMs", [T, D])
        self.QTa = it("QTa", [128, NT, 4, 128], BF16)
        self.QTb = it("QTb", [128, NT, 4, 128], BF16)
        if fused:
            self.EXk = [it("EXs%d" % k, [128, T], BF16) for k in range(4)]
            self.GXk = [it("GXs%d" % k, [256, T], BF16) for k in range(4)]
        else:
            self.EX = nc.dram_tensor("ex_out", [512, T], BF16, kind="ExternalOutput").ap()
            self.GX = self.inp("gx_in")
        base = (nc.sbuf_base + 63) // 64 * 64
        self.sb_cur = base
        self.sb_lim = nc.sbuf_top
        self.ident = self.salloc("ident", [128, 128], BF16)
        self.ones32 = self.salloc("ones32", [128, 128], F32)
        self.epsb = self.salloc("epsb", [128, 1], F32)
        self.wslot = [self.salloc("wslot0", [128, 32768], BF16), self.salloc("wslot1", [128, 32768], BF16)]
        self.local_base = self.sb_cur
        self.PS = nc.alloc_psum_tensor("ps_all", [128, 8, 512], F32)
        self.ps_cur = 0
        self.cur_x = ("xin", self.x_in)

    IN_SHAPES = {
        "x": ([T, D], F32), "att_norm": ([2, D], F32), "att_w_in": ([2, D, ATT_IN], F32), "att_sink": ([2, 8], F32),
        "att_qnorm": ([2, 64], F32), "att_knorm": ([2, 64], F32), "att_w_out": ([2, D, D], F32),
        "sgu_norm": ([2, D], F32), "sgu_w_in": ([2, D, 2 * D], F32), "sgu_ln_g": ([2, D], F32),
        "sgu_ln_b": ([2, D], F32), "sgu_w_s": ([2, 8, 128, 128], F32), "sgu_b_s": ([2, 8, 128], F32),
        "sgu_w_out": ([2, D, D], F32), "mlp_norm": ([4, D], F32), "mlp_w1": ([4, D, DFF], F32),
        "mlp_w2": ([4, DFF, D], F32), "final_norm": ([D], F32),
        "c_ident": ([128, 128], BF16), "c_rope1": ([2, 128, NT, 32], F32), "c_ropeB": ([2, 128, NT, 32], F32),
        "c_mask": ([3, 128, 384], BF16), "gx_in": ([1024, T], BF16),
    }

    def inp(self, name):
        if name not in self._ein:
            shp, dt_ = self.IN_SHAPES[name]
            self._ein[name] = self.nc.dram_tensor(name, shp, dt_, kind="ExternalInput").ap()
        return self._ein[name]

    def __getattr__(self, name):
        if name in Builder.IN_SHAPES:
            return self.inp(name)
        raise AttributeError(name)

    def ex(self, k):
        return self.EXk[k] if self.fused else self.EX[k * 128:(k + 1) * 128, :]

    def gx(self, r, k):
        if self.fused:
            return self.GXk[k][r * 128:(r + 1) * 128, :]
        return self.GX[r * 512 + k * 128:r * 512 + (k + 1) * 128, :]

    def salloc(self, name, shape, dtype):
        n = 1
        for s in shape[1:]:
            n *= s
        size = (n * DT_BYTES[dtype] + 63) // 64 * 64
        assert self.sb_cur + size <= self.sb_lim, (name, self.sb_cur, size, self.sb_lim)
        self.uid += 1
        t = self.nc.alloc_sbuf_tensor_at("%s_%d" % (name, self.uid), list(shape), dtype, offset=self.sb_cur)
        self.sb_cur += size
        return t

    def reset_local(self):
        self.sb_cur = self.local_base
        self.ps_cur = 0

    def palloc(self, name, shape, dtype=F32):
        n = 1
        for v in shape[1:]:
            n *= v
        nb = (n * DT_BYTES[dtype] + 2047) // 2048
        k = self.ps_cur
        assert k + nb <= 8, (name, k, nb)
        self.ps_cur += nb
        if nb == 1:
            ap = self.PS[:, k, :]
        else:
            ap = self.PS[:, k:k + nb, :].rearrange("p a b -> p (a b)")
        if dtype == BF16:
            ap = ap.bitcast(BF16)
        ap = ap[:, 0:n]
        if len(shape) == 3:
            ap = ap.rearrange("p (a b) -> p a b", b=shape[2])
        return ap

    def wdma(self, dst, src, tok):
        k = "w%d" % (self.wkey % 6)
        self.wkey += 1
        self.S.dma("pool", dst, src, [], [tok], k)

    def wviews(self, kind, slot):
        w = self.wslot[slot]
        if kind == "mlp":
            return (w[:, 0:16384].rearrange("p (c n) -> p c n", n=2048),
                    w[:, 16384:32768].rearrange("p (c n) -> p c n", n=1024))
        if kind == "att":
            return (w[:, 0:12288].rearrange("p (c n) -> p c n", n=1536),
                    w[0:64, 12288:28672].rearrange("p (c n) -> p c n", n=1024))
        if kind == "sgu":
            return (w[:, 0:16384].rearrange("p (c n) -> p c n", n=2048),
                    w[:, 16384:24576].rearrange("p (c n) -> p c n", n=1024),
                    w[:, 24576:25600].rearrange("p (c n) -> p c n", n=128))

    def load_weights(self, ph, slot):
        kind = ph[0]
        if kind == "mlp":
            _, l, hf = ph
            w1, w2 = self.wviews("mlp", slot)
            for kc in range(8):
                self.wdma(w1[:, kc, :], self.mlp_w1[l, kc * 128:(kc + 1) * 128, hf * 2048:(hf + 1) * 2048], ("w1", kc))
            for fc in range(16):
                r0 = (hf * 16 + fc) * 128
                self.wdma(w2[:, fc, :], self.mlp_w2[l, r0:r0 + 128, :], ("w2", fc))
        elif kind == "att":
            _, i = ph
            wi, wo = self.wviews("att", slot)
            for kc in range(8):
                self.wdma(wi[:, kc, :], self.att_w_in[i, kc * 128:(kc + 1) * 128, :], ("wi", kc))
            for h in range(16):
                self.wdma(wo[:, h, :], self.att_w_out[i, h * 64:(h + 1) * 64, :], ("wo", h))
        elif kind == "sgu":
            _, i = ph
            wi, wo, ws = self.wviews("sgu", slot)
            for kc in range(8):
                self.wdma(wi[:, kc, :], self.sgu_w_in[i, kc * 128:(kc + 1) * 128, :], ("wi", kc))
            for kc in range(8):
                self.wdma(wo[:, kc, :], self.sgu_w_out[i, kc * 128:(kc + 1) * 128, :], ("wo", kc))
            self.wdma(ws, self.sgu_w_s[i].rearrange("g p q -> p g q"), ("ws",))

    def prep_alloc(self):
        P = {}
        P["gt"] = self.salloc("gt", [128, D], F32)
        P["junk"] = self.salloc("junk", [128, D], BF16)
        P["ss"] = self.salloc("ss", [128, 16], F32)
        P["hb"] = [self.salloc("hb", [128, D], BF16) for _ in range(2)]
        P["hT"] = [self.salloc("hT", [128, 8, 512], BF16) for _ in range(2)]
        P["pT"] = self.palloc("pT", [128, 8, 128], BF16)
        return P

    def load_gain(self, P, gvec):
        self.S.dma("sp", P["gt"][:], gvec.partition_broadcast(128), [], ["gt"], "cst0")

    def prep_ew(self, P, xg, xtok, g):
        S = self.S
        s = g % 2
        ss = P["ss"][:, s * 8:(s + 1) * 8]
        for j in range(4):
            S.add("act", lambda e, j=j: e.activation(out=P["junk"][:], in_=xg[:, j, :], func=AF.Square,
                                                     accum_out=ss[:, j:j + 1]),
                  [xtok], ["junk", ("ss", j)])
        S.add("act", lambda e: e.activation(out=ss[:, 4:8], in_=ss[:, 0:4], func=AF.Ln, scale=1.0 / D,
                                            bias=self.epsb[:, 0:1]),
              [("ss", j) for j in range(4)], [("rs", s)])
        S.add("act", lambda e: e.activation(out=ss[:, 4:8], in_=ss[:, 4:8], func=AF.Exp, scale=-0.5),
              [("rs", s)], [("rs", s)])

    def prep_tr(self, P, xg, xtok, g, evac="act", part=None):
        S = self.S
        s = g % 2
        hT = P["hT"][s]

        def hb_op(j):
            hb = P["hb"][j % 2]
            S.add("dve", lambda e, j=j, hb=hb: e.scalar_tensor_tensor(out=hb[:], in0=xg[:, j, :],
                                                                      scalar=P["ss"][:, s * 8 + 4 + j:s * 8 + 5 + j], in1=P["gt"][:],
                                                                      op0=ALU.mult, op1=ALU.mult),
                  [xtok, ("rs", s), "gt"], [("hb", j % 2)])

        if part in (None, "early"):
            hb_op(0)
            hb_op(1)
        if part == "early":
            return
        for j in range(4):
            hb = P["hb"][j % 2]
            if j >= 2:
                hb_op(j)
            for c in range(8):
                S.add("pe", lambda e, c=c, hb=hb: e.transpose(out=P["pT"][:, c, :], in_=hb[:, c * 128:(c + 1) * 128],
                                                              identity=self.ident[:]),
                      [("hb", j % 2)], ["pT"])
            if evac == "act":
                S.add("act", lambda e, j=j: e.copy(out=hT[:, :, j * 128:(j + 1) * 128], in_=P["pT"][:]),
                      ["pT"], [("hT", s)])
            else:
                S.add("dve", lambda e, j=j: e.tensor_copy(out=hT[:, :, j * 128:(j + 1) * 128], in_=P["pT"][:]),
                      ["pT"], [("hT", s)])

    def xview(self, ap, g):
        return ap[g * 512:(g + 1) * 512, :].rearrange("(j p) d -> p j d", p=128)

    def phase_mlp(self, ph, slot, fuse_final=False):
        _, l, hf = ph
        S = self.S
        w1, w2 = self.wviews("mlp", slot)
        P = self.prep_alloc()
        bufs = [self.salloc("xbuf", [128, 4, D], F32) for _ in range(2)]
        aT = self.salloc("aT", [128, 16, 512], BF16)
        rb = [self.salloc("rb", [128, 512], BF16) for _ in range(2)]
        pm = [self.palloc("pm", [128, 512]) for _ in range(2)]
        po = [self.palloc("po", [128, 2, 512]) for _ in range(2)]
        srcname, src = self.cur_x
        self.load_gain(P, self.mlp_norm[l])
        if hf == 0:
            dst = self.XM
        else:
            dst = self.X
        if fuse_final:
            dst = self.out
            gtf = self.wslot[1 - slot][:, 0:2048].bitcast(F32)
            fs = self.salloc("fs", [128, 8], F32)
            S.dma("sp", gtf, self.final_norm.partition_broadcast(128), [], ["gtf"], "cst1")

        def xbuf(g):
            return (bufs[g % 2], ("xb", g % 2)) if hf == 0 else (bufs[0], ("xb", 0))

        def load_x(g):
            b, tok = xbuf(g)
            S.dma("sp", b[:], self.xview(src, g), [], [tok], "ldx%d" % (g % 2))

        def load_m(g):
            S.dma("sp", bufs[1][:], self.xview(self.XM, g), [], [("xb", 1)], "ldm")

        load_x(0)
        if hf == 1:
            load_m(0)
        b0, t0 = xbuf(0)
        self.prep_ew(P, b0, t0, 0)
        self.prep_tr(P, b0, t0, 0)
        for g in range(NG):
            hT = P["hT"][g % 2]
            for fc in range(16):
                ps = pm[fc % 2]
                for kc in range(8):
                    S.add("pe", lambda e, ps=ps, kc=kc, fc=fc, hT=hT: e.matmul(
                        ps[:], lhsT=w1[:, kc, fc * 128:(fc + 1) * 128], rhs=hT[:, kc, :],
                        start=(kc == 0), stop=(kc == 7)),
                        [("w1", kc), ("hT", g % 2)], [("pm", fc % 2)])
                r = rb[fc % 2]
                S.add("act", lambda e, ps=ps, r=r: e.activation(out=r[:], in_=ps[:], func=AF.Relu),
                      [("pm", fc % 2)], [("rb", fc % 2)])
                S.add("dve", lambda e, r=r, fc=fc: e.tensor_tensor(out=aT[:, fc, :], in0=r[:], in1=r[:], op=ALU.mult),
                      [("rb", fc % 2)], [("aT", fc)])
            if g + 1 < NG:
                load_x(g + 1)
                nb, nt = xbuf(g + 1)
                self.prep_ew(P, nb, nt, g + 1)
                self.prep_tr(P, nb, nt, g + 1, part="early")
            acc, acctok = (xbuf(g) if hf == 0 else (bufs[1], ("xb", 1)))
            for j in range(4):
                if j == 2 and g + 1 < NG:
                    nb, nt = xbuf(g + 1)
                    self.prep_tr(P, nb, nt, g + 1, part="late")
                pso = po[j % 2]
                for n in range(2):
                    for fc in range(16):
                        S.add("pe", lambda e, pso=pso, n=n, fc=fc, j=j: e.matmul(
                            pso[:, n, :], lhsT=aT[:, fc, j * 128:(j + 1) * 128], rhs=w2[:, fc, n * 512:(n + 1) * 512],
                            start=(fc == 0), stop=(fc == 15)),
                            [("aT", fc), ("w2", fc)], [("po", j % 2)])
                S.add("dve", lambda e, pso=pso, j=j, acc=acc: e.tensor_tensor(
                    out=acc[:, j, :], in0=pso.rearrange("p a b -> p (a b)"), in1=acc[:, j, :], op=ALU.add),
                    [("po", j % 2), acctok], [acctok])
            if fuse_final:
                for j in range(4):
                    S.add("act", lambda e, j=j, acc=acc: e.activation(out=P["junk"][:], in_=acc[:, j, :], func=AF.Square,
                                                                      accum_out=fs[:, j:j + 1]),
                          [acctok], ["junk", ("fs", j)])
                S.add("act", lambda e: e.activation(out=fs[:, 4:8], in_=fs[:, 0:4], func=AF.Ln, scale=1.0 / D,
                                                    bias=self.epsb[:, 0:1]), [("fs", j) for j in range(4)], ["frs"])
                S.add("act", lambda e: e.activation(out=fs[:, 4:8], in_=fs[:, 4:8], func=AF.Exp, scale=-0.5),
                      ["frs"], ["frs"])
                for j in range(4):
                    S.add("dve", lambda e, j=j, acc=acc: e.scalar_tensor_tensor(
                        out=acc[:, j, :], in0=acc[:, j, :], scalar=fs[:, 4 + j:5 + j], in1=gtf,
                        op0=ALU.mult, op1=ALU.mult), [acctok, "frs", "gtf"], [acctok])
            S.dma("sp", self.xview(dst, g), acc[:], [acctok], [], "stx%d" % (g % 2))
            if hf == 1 and g + 1 < NG:
                load_m(g + 1)
        if hf == 1:
            self.cur_x = ("X", self.X)

    def phase_final(self):
        S = self.S
        P = self.prep_alloc()
        bufs = [self.salloc("xbuf", [128, 4, D], F32) for _ in range(2)]
        _, src = self.cur_x
        self.load_gain(P, self.final_norm)
        for g in range(NG):
            b = bufs[g % 2]
            tok = ("xb", g % 2)
            S.dma("sp", b[:], self.xview(src, g), [], [tok], "ldx%d" % (g % 2))
            self.prep_ew(P, b, tok, g)
            for j in range(4):
                S.add("dve", lambda e, j=j, b=b, g=g: e.scalar_tensor_tensor(
                    out=b[:, j, :], in0=b[:, j, :], scalar=P["ss"][:, (g % 2) * 8 + 4 + j:(g % 2) * 8 + 5 + j], in1=P["gt"][:],
                    op0=ALU.mult, op1=ALU.mult), [tok, ("rs", g % 2), "gt"], [tok])
            S.dma("sp", self.xview(self.out, g), b[:], [tok], [], "stx%d" % (g % 2))

    def phase_copy_out(self):
        _, src = self.cur_x
        for g in range(NG):
            self.S.dma("sp", self.out[g * 512:(g + 1) * 512, :], src[g * 512:(g + 1) * 512, :], [], [], "cp%d" % (g % 4))

    def phase_sgu(self, ph, slot):
        _, i = ph
        S = self.S
        wi, wo, ws = self.wviews("sgu", slot)
        wsl = self.wslot[slot]
        P = self.prep_alloc()
        xb = self.salloc("xbuf", [128, 4, D], F32)
        lng = wsl[:, 25600:27648].bitcast(F32)
        lnb = wsl[:, 27648:29696].bitcast(F32)
        bst = self.salloc("bst", [128, 8], F32)
        wsT = self.salloc("wsT", [128, 8, 128], BF16)
        tmp = self.salloc("tmp", [128, D], F32)
        sets = []
        for k in range(2):
            sets.append(dict(
                u=self.salloc("u", [128, D], BF16), v=self.salloc("v", [128, D], F32),
                vnb=self.salloc("vnb", [128, D], BF16), yb=self.salloc("yb", [128, D], BF16),
                yT=self.salloc("yT", [128, 8, 128], BF16), st=self.salloc("st", [128, 4, 6], F32),
                mv=self.salloc("mv", [128, 4], F32)))
        pz = [self.palloc("pz", [128, 512]) for _ in range(4)]
        pq = self.palloc("pq", [128, 2, 512])
        _, src = self.cur_x
        self.load_gain(P, self.sgu_norm[i])
        S.dma("sp", lng, self.sgu_ln_g[i].partition_broadcast(128), [], ["lng"], "cst1")
        S.dma("sp", lnb, self.sgu_ln_b[i].partition_broadcast(128), [], ["lnb"], "cst2")
        S.dma("sp", bst[:], self.sgu_b_s[i].rearrange("g p -> p g"), [], ["bst"], "cst3", slow=True)
        for gi in range(8):
            S.add("pe", lambda e, gi=gi: e.transpose(out=P["pT"][:, gi, :], in_=ws[:, gi, :], identity=self.ident[:]),
                  [("ws",)], ["pT"])
        S.add("act", lambda e: e.copy(out=wsT[:], in_=P["pT"][:]), ["pT"], ["wsT"])
        tok = ("xb", 0)
        pzc = [0]

        def tile_steps(g, j, hT):
            B = sets[j % 2]
            k_ = j % 2
            u, v, vnb, yb, yT, st, mv = B["u"], B["v"], B["vnb"], B["yb"], B["yT"], B["st"], B["mv"]
            for n in range(4):
                pb = pzc[0] % 4
                pzc[0] += 1
                for kc in range(8):
                    S.add("pe", lambda e, n=n, kc=kc, pb=pb: e.matmul(
                        pz[pb][:, :], lhsT=hT[:, kc, j * 128:(j + 1) * 128], rhs=wi[:, kc, n * 512:(n + 1) * 512],
                        start=(kc == 0), stop=(kc == 7)), [("hT", g % 2), ("wi", kc)], [("pz", pb)])
                dst = u[:, n * 512:(n + 1) * 512] if n < 2 else v[:, (n - 2) * 512:(n - 1) * 512]
                S.add("act", lambda e, dst=dst, pb=pb: e.activation(out=dst, in_=pz[pb][:, :], func=AF.Gelu_apprx_tanh),
                      [("pz", pb)], [("u", k_) if n < 2 else ("v", k_)])
            yield
            for k in range(2):
                S.add("dve", lambda e, k=k: e.bn_stats(out=st[:, k, :], in_=v[:, k * 512:(k + 1) * 512]),
                      [("v", k_)], [("st", k_, k)])
            S.add("dve", lambda e: e.bn_aggr(out=mv[:, 0:2], in_=st[:, 0:2, :]), [("st", k_, 0), ("st", k_, 1)], [("mv", k_)])
            yield
            S.add("act", lambda e: e.activation(out=mv[:, 2:3], in_=mv[:, 1:2], func=AF.Ln, bias=self.epsb[:, 0:1]),
                  [("mv", k_)], [("mv2", k_)])
            S.add("act", lambda e: e.activation(out=mv[:, 2:3], in_=mv[:, 2:3], func=AF.Exp, scale=-0.5),
                  [("mv2", k_)], [("mv2", k_)])
            yield
            S.add("dve", lambda e: e.tensor_scalar(out=tmp[:], in0=v[:], scalar1=mv[:, 0:1], scalar2=mv[:, 2:3],
                                                  op0=ALU.subtract, op1=ALU.mult), [("v", k_), ("mv", k_), ("mv2", k_)], ["tmp"])
            S.add("dve", lambda e: e.tensor_tensor(out=tmp[:], in0=tmp[:], in1=lng, op=ALU.mult),
                  ["tmp", "lng"], ["tmp"])
            S.add("dve", lambda e: e.tensor_tensor(out=vnb[:], in0=tmp[:], in1=lnb, op=ALU.add),
                  ["tmp", "lnb"], [("vnb", k_)])
            yield
            for gi in range(8):
                S.add("pe", lambda e, gi=gi: e.matmul(
                    pq[:, gi // 4, (gi % 4) * 128:(gi % 4 + 1) * 128], lhsT=wsT[:, gi, :],
                    rhs=vnb[:, gi * 128:(gi + 1) * 128], start=True, stop=True), ["wsT", ("vnb", k_)], ["pq"])
            S.add("dve", lambda e: e.tensor_tensor(
                out=tmp[:].rearrange("p (g d) -> p g d", d=128),
                in0=pq.rearrange("p a (g d) -> p (a g) d", d=128),
                in1=bst[:, 0:8].unsqueeze(2).to_broadcast([128, 8, 128]), op=ALU.add), ["pq", "bst"], ["tmp"])
            S.add("dve", lambda e: e.tensor_tensor(out=yb[:], in0=tmp[:], in1=u[:], op=ALU.mult),
                  ["tmp", ("u", k_)], [("yb", k_)])
            yield
            for c in range(8):
                S.add("pe", lambda e, c=c: e.transpose(out=P["pT"][:, c, :], in_=yb[:, c * 128:(c + 1) * 128],
                                                       identity=self.ident[:]), [("yb", k_)], ["pT"])
            S.add("act", lambda e: e.copy(out=yT[:], in_=P["pT"][:]), ["pT"], [("yT", k_)])
            yield
            for n in range(2):
                for kc in range(8):
                    S.add("pe", lambda e, n=n, kc=kc: e.matmul(
                        pq[:, n, :], lhsT=yT[:, kc, :], rhs=wo[:, kc, n * 512:(n + 1) * 512],
                        start=(kc == 0), stop=(kc == 7)), [("yT", k_), ("wo", kc)], ["pq"])
            S.add("dve", lambda e: e.tensor_tensor(
                out=xb[:, j, :], in0=pq.rearrange("p a b -> p (a b)"), in1=xb[:, j, :], op=ALU.add),
                ["pq", tok], [tok])
            yield

        for g in range(NG):
            S.dma("sp", xb[:], self.xview(src, g), [], [tok], "ldx0")
            self.prep_ew(P, xb, tok, g)
            self.prep_tr(P, xb, tok, g)
            hT = P["hT"][g % 2]
            todo = [tile_steps(g, j, hT) for j in range(4)]
            active = [todo.pop(0), todo.pop(0)]
            while active:
                for g_ in list(active):
                    try:
                        next(g_)
                    except StopIteration:
                        active.remove(g_)
                        if todo:
                            active.append(todo.pop(0))
            S.dma("sp", self.xview(self.X, g), xb[:], [tok], [], "stx%d" % (g % 2))
        self.cur_x = ("X", self.X)

    def phase_prep_att(self, ph, slot):
        _, i = ph
        S = self.S
        wi, wo = self.wviews("att", slot)
        wsl = self.wslot[slot]
        P = self.prep_alloc()
        xb = self.salloc("xbuf", [128, 4, D], F32)
        rope1 = self.salloc("rope1", [128, 2, 4, 32], F32)
        ropeB = self.salloc("ropeB", [128, 2, 4, 32], F32)
        gB = self.salloc("gB", [128, 10, 64], F32)
        pjs = [self.salloc("pjs", [128, ATT_IN], F32), wsl[:, 28672:31744].bitcast(F32)]
        t1 = self.salloc("t1", [128, 640], F32)
        t2 = self.salloc("t2", [128, 320], F32)
        xn = self.salloc("xn", [128, 640], F32)
        p1 = wsl[:, 31744:32384].bitcast(F32)
        p2 = self.salloc("p2", [128, 320], F32)
        sbs = [self.salloc("sb", [128, 32], F32) for _ in range(2)]
        rots = [self.salloc("rot", [128, 10, 128], BF16) for _ in range(2)]
        qst = self.salloc("qst", [128, 4, 10, 128], BF16)
        vst = self.salloc("vst", [128, 4, 2, 128], BF16)
        pj = [self.palloc("pj", [128, 512]) for _ in range(4)]
        pq = self.palloc("pqT", [128, 10, 128], BF16)
        _, src = self.cur_x
        self.load_gain(P, self.att_norm[i])
        for h in range(10):
            gv = self.att_qnorm[i] if h < 8 else self.att_knorm[i]
            S.dma("sp", gB[:, h, :], gv.partition_broadcast(128), [], [("gB", h)], "cst%d" % (5 + h % 2))
        gBtok = [("gB", h) for h in range(10)]
        ropetok = ["rope1", "rope1x", "ropeB", "ropeBx"]
        KTb, KTa = self.ex(0), self.ex(1)
        Vb = self.ex(2).rearrange("r (t d) -> (r t) d", d=128)
        Va = self.ex(3).rearrange("r (t d) -> (r t) d", d=128)
        tok = ("xb", 0)
        pjc = [0]

        def tile_steps(g, j, hT):
            k_ = j % 2
            pj_s, rot, sb = pjs[k_], rots[k_], sbs[k_]
            for n in range(3):
                pb = pjc[0] % 4
                pjc[0] += 1
                for kc in range(8):
                    S.add("pe", lambda e, n=n, kc=kc, pb=pb: e.matmul(
                        pj[pb][:, :], lhsT=hT[:, kc, j * 128:(j + 1) * 128], rhs=wi[:, kc, n * 512:(n + 1) * 512],
                        start=(kc == 0), stop=(kc == 7)), [("hT", g % 2), ("wi", kc)], [("pj", pb)])
                S.add("act", lambda e, n=n, pb=pb: e.copy(out=pj_s[:, n * 512:(n + 1) * 512], in_=pj[pb][:, :]),
                      [("pj", pb)], [("pjs", k_)])
            S.add("act", lambda e: e.copy(out=vst[:, j, 0, :], in_=pj_s[:, 640:768]), [("pjs", k_)], [("vst", j)])
            S.add("act", lambda e: e.copy(out=vst[:, j, 1, :], in_=pj_s[:, 1408:1536]), [("pjs", k_)], [("vst", j)])
            yield
            cosA = rope1[:, 0, j, :]
            sinA = rope1[:, 1, j, :]
            segs = []
            for kv in range(2):
                segs.append((pj_s[:, kv * 256:(kv + 1) * 256].rearrange("p (i d) -> p i d", d=64),
                             rot[:, 0:4, kv * 64:(kv + 1) * 64], 4))
            segs.append((pj_s[:, 512:640].rearrange("p (i d) -> p i d", d=64),
                         rot[:, 4, :].rearrange("p (i d) -> p i d", d=64), 2))
            rtA = [("pjs", k_)] + ropetok
            plan = []
            off = 0
            for si, (xi, xo, nh) in enumerate(segs):
                cb = cosA.unsqueeze(1).to_broadcast([128, nh, 32])
                sbb = sinA.unsqueeze(1).to_broadcast([128, nh, 32])
                x1, x2 = xi[:, :, 0:32], xi[:, :, 32:64]
                a1 = p1[:, off:off + nh * 32].rearrange("p (i d) -> p i d", d=32)
                a2 = p2[:, off:off + nh * 32].rearrange("p (i d) -> p i d", d=32)
                off += nh * 32
                T1, T2 = ("p1", si), ("p2", si)
                plan.append([
                    (lambda e, a1=a1, x1=x1, cb=cb: e.tensor_tensor(out=a1, in0=x1, in1=cb, op=ALU.mult), rtA, [T1]),
                    (lambda e, a2=a2, x2=x2, sbb=sbb: e.tensor_tensor(out=a2, in0=x2, in1=sbb, op=ALU.mult), rtA, [T2]),
                    (lambda e, xo=xo, a1=a1, a2=a2: e.tensor_tensor(out=xo[:, :, 0:32], in0=a1, in1=a2, op=ALU.subtract),
                     [T1, T2], [("rotA", k_)]),
                    (lambda e, a1=a1, x2=x2, cb=cb: e.tensor_tensor(out=a1, in0=x2, in1=cb, op=ALU.mult), rtA, [T1]),
                    (lambda e, a2=a2, x1=x1, sbb=sbb: e.tensor_tensor(out=a2, in0=x1, in1=sbb, op=ALU.mult), rtA, [T2]),
                    (lambda e, xo=xo, a1=a1, a2=a2: e.tensor_tensor(out=xo[:, :, 32:64], in0=a1, in1=a2, op=ALU.add),
                     [T1, T2], [("rotA", k_)]),
                ])
            for step in range(6):
                for seg_ops in plan:
                    f, r_, w_ = seg_ops[step]
                    S.add("pool", f, r_, w_)
            xbv = pj_s[:, 768:1408]
            S.add("dve", lambda e: e.tensor_tensor(out=t1[:], in0=xbv, in1=xbv, op=ALU.mult), [("pjs", k_)], ["t1"])
            S.add("dve", lambda e: e.tensor_reduce(out=sb[:, 0:10], in_=t1[:].rearrange("p (h d) -> p h d", d=64),
                                                  axis=AX.X, op=ALU.add), ["t1"], [("sb", k_)])
            yield
            S.add("act", lambda e: e.activation(out=sb[:, 16:26], in_=sb[:, 0:10], func=AF.Ln, scale=1.0 / 64,
                                                bias=self.epsb[:, 0:1]), [("sb", k_)], [("sb2", k_)])
            S.add("act", lambda e: e.activation(out=sb[:, 16:26], in_=sb[:, 16:26], func=AF.Exp, scale=-0.5),
                  [("sb2", k_)], [("sb2", k_)])
            yield
            S.add("dve", lambda e: e.tensor_tensor(
                out=t1[:].rearrange("p (h d) -> p h d", d=64), in0=xbv.rearrange("p (h d) -> p h d", d=64),
                in1=sb[:, 16:26].unsqueeze(2).to_broadcast([128, 10, 64]), op=ALU.mult), [("pjs", k_), ("sb2", k_), "t1"], ["t1"])
            S.add("dve", lambda e: e.tensor_tensor(out=xn[:], in0=t1[:], in1=gB[:].rearrange("p h d -> p (h d)"),
                                                  op=ALU.mult), ["t1"] + gBtok, ["xn"])
            cosB = ropeB[:, 0, j, :].rearrange("p (a d) -> p a d", d=16)
            sinB = ropeB[:, 1, j, :].rearrange("p (a d) -> p a d", d=16)
            segs = []
            for kv in range(2):
                segs.append((xn[:, kv * 256:(kv + 1) * 256].rearrange("p (i a c d) -> p i a c d", a=2, c=2, d=16),
                             rot[:, 5:9, kv * 64:(kv + 1) * 64].rearrange("p i (a c d) -> p i a c d", a=2, c=2, d=16), 4))
            segs.append((xn[:, 512:640].rearrange("p (i a c d) -> p i a c d", a=2, c=2, d=16),
                         rot[:, 9, :].rearrange("p (i a c d) -> p i a c d", a=2, c=2, d=16), 2))
            plan = []
            off = 0
            for si, (xi, xo, nh) in enumerate(segs):
                cb = cosB.unsqueeze(1).to_broadcast([128, nh, 2, 16])
                sbb = sinB.unsqueeze(1).to_broadcast([128, nh, 2, 16])
                x1, x2 = xi[:, :, :, 0, :], xi[:, :, :, 1, :]
                a1 = t1[:, off:off + nh * 32].rearrange("p (i a d) -> p i a d", a=2, d=16)
                a2 = t2[:, off:off + nh * 32].rearrange("p (i a d) -> p i a d", a=2, d=16)
                off += nh * 32
                rt = ["xn"] + ropetok
                T1, T2 = ("t1", si), ("t2", si)
                first = ["t1"] if True else []
                plan.append([
                    (lambda e, a1=a1, x1=x1, cb=cb: e.tensor_tensor(out=a1, in0=x1, in1=cb, op=ALU.mult), rt + ["t1"], [T1]),
                    (lambda e, a2=a2, x2=x2, sbb=sbb: e.tensor_tensor(out=a2, in0=x2, in1=sbb, op=ALU.mult), rt, [T2]),
                    (lambda e, xo=xo, a1=a1, a2=a2: e.tensor_tensor(out=xo[:, :, :, 0, :], in0=a1, in1=a2, op=ALU.subtract),
                     [T1, T2], [("rotB", k_)]),
                    (lambda e, a1=a1, x2=x2, cb=cb: e.tensor_tensor(out=a1, in0=x2, in1=cb, op=ALU.mult), rt, [T1]),
                    (lambda e, a2=a2, x1=x1, sbb=sbb: e.tensor_tensor(out=a2, in0=x1, in1=sbb, op=ALU.mult), rt, [T2]),
                    (lambda e, xo=xo, a1=a1, a2=a2: e.tensor_tensor(out=xo[:, :, :, 1, :], in0=a1, in1=a2, op=ALU.add),
                     [T1, T2], [("rotB", k_)]),
                ])
            for step in range(6):
                for seg_ops in plan:
                    f, r_, w_ = seg_ops[step]
                    S.add("dve", f, r_, w_)
            S.add("dve", lambda e: e.memset(sb[:, 30:31], 0.0), [("t1", 0), ("t1", 1), ("t1", 2), ("t2", 0)], ["t1"])
            yield
            for c in range(10):
                S.add("pe", lambda e, c=c: e.transpose(out=pq[:, c, :], in_=rot[:, c, :], identity=self.ident[:]),
                      [("rotA", k_), ("rotB", k_)], ["pqT"])
            S.add("act", lambda e: e.copy(out=qst[:, j, :, :], in_=pq), ["pqT"], [("qst", j)])
            yield

        def group_prologue(g):
            S.dma("sp", xb[:], self.xview(src, g), [], [tok], "ldx%d" % (g % 2))
            yield
            self.prep_ew(P, xb, tok, g)
            yield
            yield
            self.prep_tr(P, xb, tok, g)
            yield

        for _ in group_prologue(0):
            pass
        for g in range(NG):
            for k in range(2):
                S.dma("sp", rope1[:, k, :, :], self.c_rope1[k][:, g * 4:(g + 1) * 4, :], [],
                      ["rope1"] if k else ["rope1x"], "cst%d" % (1 + k))
                S.dma("sp", ropeB[:, k, :, :], self.c_ropeB[k][:, g * 4:(g + 1) * 4, :], [],
                      ["ropeB"] if k else ["ropeBx"], "cst%d" % (3 + k))
            hT = P["hT"][g % 2]
            todo = [tile_steps(g, j, hT) for j in range(4)]
            active = [todo.pop(0), todo.pop(0)]
            side = group_prologue(g + 1) if g + 1 < NG else None
            while active:
                for g_ in list(active):
                    try:
                        next(g_)
                    except StopIteration:
                        active.remove(g_)
                        if todo:
                            active.append(todo.pop(0))
                if side is not None:
                    try:
                        next(side)
                    except StopIteration:
                        side = None
            if side is not None:
                for _ in side:
                    pass
            qtok = [("qst", j) for j in range(4)]
            t0 = g * 4
            S.dma("sp", self.QTa[:, t0:t0 + 4, :, :], qst[:, :, 0:4, :], qtok, [], "sq0")
            S.dma("sp", self.QTb[:, t0:t0 + 4, :, :], qst[:, :, 5:9, :], qtok, [], "sq1")
            S.dma("sp", KTa[:, g * 512:(g + 1) * 512].rearrange("p (j k) -> p j k", k=128), qst[:, :, 4, :], qtok, [], "sq2")
            S.dma("sp", KTb[:, g * 512:(g + 1) * 512].rearrange("p (j k) -> p j k", k=128), qst[:, :, 9, :], qtok, [], "sq3")
            vtok = [("vst", j) for j in range(4)]
            S.dma("sp", Va[g * 512:(g + 1) * 512, :].rearrange("(j p) d -> p j d", p=128), vst[:, :, 0, :], vtok, [], "sq4")
            S.dma("sp", Vb[g * 512:(g + 1) * 512, :].rearrange("(j p) d -> p j d", p=128), vst[:, :, 1, :], vtok, [], "sq5")

    def phase_exchange(self):
        for k in range(4):
            self.S.add("pool", lambda e, k=k: e.collective_compute(
                "AllGather", ALU.bypass, replica_groups=[[0, 1], [2, 3], [4, 5], [6, 7]],
                ins=[self.EXk[k].opt()], outs=[self.GXk[k].opt()]), [], [], dma_key="cc%d" % k, dma_inc=1)

    def phase_att(self, ph, slot):
        _, i = ph
        S = self.S
        wi, wo = self.wviews("att", slot)
        wsl = self.wslot[slot]
        KTb_s = self.salloc("KTb", [128, 8192], BF16)
        Vb_s = self.salloc("Vb", [128, 64, 2, 65], BF16)
        KTa_s = self.salloc("KTa", [128, 34, 128], BF16)
        Va_s = self.salloc("Va", [128, 34, 128], BF16)
        msk = self.salloc("msk", [128, 3, 384], BF16)
        sink = self.salloc("sink", [128, 8], F32)
        gq = self.salloc("gq", [128, 2, 64], F32)
        nsh = self.salloc("nsh", [128, 4], F32)
        qa = self.salloc("qa", [128, 4, 128], BF16)
        qz = [[self.salloc("qz", [128, 512], BF16), wsl[:, 30720:31232]],
              [wsl[:, 31232:31744], wsl[:, 31744:32256]]]
        NP = 6
        pts = [self.salloc("pts", [128, 512], BF16) for _ in range(NP)]
        OTs = [self.salloc("OT", [64, 16, 128], BF16),
               wsl[0:64, 28672:30720].rearrange("p (h k) -> p h k", k=128)]
        sm = self.salloc("sm", [128, 2, 384], F32)
        pe_ = self.salloc("pexp", [128, 2, 384], BF16)
        pn = self.salloc("pn", [128, 2, 384], BF16)
        PTs = self.salloc("PTs", [128, 2, 3, 128], BF16)
        st = self.salloc("stt", [128, 16], F32)
        xt = self.salloc("xt", [128, D], F32)
        rec = wsl[:, 12288:13312].bitcast(F32)
        bcs = self.salloc("bcs", [64, 512], F32)
        NS = 4
        psS = [self.palloc("psS", [128, 512]) for _ in range(NS)]
        psO = [self.palloc("psO", [128, 512]) for _ in range(2)]
        psB = self.palloc("psB", [128, 512])
        psX = self.palloc("psX", [128, 512])
        psW = psX
        psA = psX
        _, src = self.cur_x
        KTa_own = self.ex(1)
        Va_own = self.ex(3).rearrange("r (t d) -> (r t) d", d=128)
        for r in range(2):
            S.dma("sp", KTb_s[:, r * T:(r + 1) * T], self.gx(r, 0), [], [("KTb", r)], "ca%d" % r)
            vsrc = self.gx(r, 2).rearrange("r (t d) -> (r t) d", d=128)
            for q in range(4):
                c0 = r * 32 + q * 8
                for kv in range(2):
                    S.dma("sp", Vb_s[:, c0:c0 + 8, kv, 0:64],
                          vsrc[q * 1024:(q + 1) * 1024, kv * 64:(kv + 1) * 64].rearrange("(c p) d -> p c d", p=128),
                          [], [("Vb", r, q, kv)], "cb%d" % (q * 2 + kv))
        S.add("pool", lambda e: e.memset(Vb_s[:, :, :, 64:65], 1.0), [], ["Vb1"])
        S.add("pool", lambda e: e.memset(self.ones32[:], 1.0), [], ["ones32"])
        for par in range(2):
            for kv in range(2):
                S.add("pool", lambda e, par=par, kv=kv: e.memset(qz[par][kv][:, :], 0.0), [], [("qz", par)])
        kvtok = [("KTb", 0), ("KTb", 1), "Vb1"] + [("Vb", r, q, kv) for r in range(2) for q in range(4) for kv in range(2)]
        S.dma("sp", KTa_s[:, 1:33, :], KTa_own.rearrange("p (c k) -> p c k", k=128), [], ["KTa0"], "ce0")
        S.dma("sp", KTa_s[:, 0, :], self.gx(0, 1)[:, T - 128:T], [], ["KTa1"], "ce1")
        S.dma("sp", KTa_s[:, 33, :], self.gx(1, 1)[:, 0:128], [], ["KTa2"], "ce2")
        S.dma("sp", Va_s[:, 1:33, :], Va_own.rearrange("(c p) d -> p c d", p=128), [], ["Va0"], "ce3")
        va0 = self.gx(0, 3).rearrange("r (t d) -> (r t) d", d=128)
        va1 = self.gx(1, 3).rearrange("r (t d) -> (r t) d", d=128)
        S.dma("sp", Va_s[:, 0, :], va0[T - 128:T, :], [], ["Va1"], "ce4")
        S.dma("sp", Va_s[:, 33, :], va1[0:128, :], [], ["Va2"], "ce5")
        atok = ["KTa0", "KTa1", "KTa2", "Va0", "Va1", "Va2"]
        S.dma("sp", msk[:], self.c_mask.rearrange("m p k -> p m k"), [], ["msk"], "cd0")
        S.dma("sp", sink[:], self.att_sink[i].partition_broadcast(128), [], ["sink"], "cd1")
        S.dma("sp", gq[:, 0, :], self.att_qnorm[i].partition_broadcast(128), [], ["gq0"], "cd2")
        S.dma("sp", gq[:, 1, :], self.att_knorm[i].partition_broadcast(128), [], ["gq1"], "cd3")
        S.add("dve", lambda e: e.tensor_reduce(out=nsh[:, 0:2], in_=gq[:], axis=AX.X, op=ALU.max,
                                              apply_absolute_value=True), ["gq0", "gq1"], ["nsh0"])
        S.add("dve", lambda e: e.tensor_tensor(out=nsh[:, 2:3], in0=nsh[:, 0:1], in1=nsh[:, 1:2], op=ALU.mult),
              ["nsh0"], ["nsh1"])
        S.add("dve", lambda e: e.tensor_scalar(out=nsh[:, 3:4], in0=nsh[:, 2:3], scalar1=-8.0, scalar2=None,
                                              op0=ALU.mult), ["nsh1"], ["nsh"])

        def win_steps(b):
            OT = OTs[b % 2]
            S.dma("sp", qa[:], self.QTa[:, b, :, :], [], ["qa"], "qa")
            mi = 1 if b == 0 else (2 if b == NT - 1 else 0)
            for kv in range(2):
                for hp in range(2):
                    h0 = kv * 4 + hp * 2
                    for hh in range(2):
                        ih = hp * 2 + hh
                        S.add("pe", lambda e, kv=kv, ih=ih, b=b: e.matmul(
                            psA[:, 0:384], lhsT=qa[kv * 64:(kv + 1) * 64, ih, :],
                            rhs=KTa_s[kv * 64:(kv + 1) * 64, b:b + 3, :].rearrange("p c k -> p (c k)"),
                            start=True, stop=True), ["qa"] + atok, ["psX"])
                        S.add("dve", lambda e, hh=hh, mi=mi: e.scalar_tensor_tensor(
                            out=sm[:, hh, :], in0=psA[:, 0:384], scalar=0.125, in1=msk[:, mi, :],
                            op0=ALU.mult, op1=ALU.add), ["psX", "msk"], ["sm"])
                    yield
                    S.add("dve", lambda e: e.tensor_reduce(out=st[:, 0:2], in_=sm[:], axis=AX.X, op=ALU.max),
                          ["sm"], ["st0"])
                    S.add("dve", lambda e, h0=h0: e.tensor_tensor(out=st[:, 2:4], in0=st[:, 0:2], in1=sink[:, h0:h0 + 2],
                                                                 op=ALU.max), ["st0", "sink"], ["st1"])
                    S.add("dve", lambda e: e.tensor_scalar(out=st[:, 4:6], in0=st[:, 2:4], scalar1=-1.0, scalar2=None,
                                                          op0=ALU.mult), ["st1"], ["st2"])
                    S.add("dve", lambda e, h0=h0: e.tensor_tensor(out=st[:, 6:8], in0=sink[:, h0:h0 + 2], in1=st[:, 2:4],
                                                                 op=ALU.subtract), ["st1", "sink"], ["st3"])
                    yield
                    yield
                    for hh in range(2):
                        S.add("act", lambda e, hh=hh: e.activation(out=pe_[:, hh, :], in_=sm[:, hh, :], func=AF.Exp,
                                                                   bias=st[:, 4 + hh:5 + hh], accum_out=st[:, 8 + hh:9 + hh]),
                              ["sm", "st2"], ["pexp", ("den", hh)])
                    S.add("act", lambda e: e.activation(out=st[:, 10:12], in_=st[:, 6:8], func=AF.Exp), ["st3"], ["st4"])
                    yield
                    S.add("dve", lambda e: e.tensor_tensor(out=st[:, 12:14], in0=st[:, 8:10], in1=st[:, 10:12], op=ALU.add),
                          [("den", 0), ("den", 1), "st4"], ["st5"])
                    S.add("dve", lambda e: e.reciprocal(out=st[:, 14:16], in_=st[:, 12:14]), ["st5"], ["st6"])
                    for hh in range(2):
                        S.add("dve", lambda e, hh=hh: e.tensor_scalar(out=pn[:, hh, :], in0=pe_[:, hh, :],
                                                                     scalar1=st[:, 14 + hh:15 + hh], scalar2=None,
                                                                     op0=ALU.mult), ["pexp", "st6"], ["pn"])
                    yield
                    yield
                    pT = psB[:, 0:384].bitcast(BF16).rearrange("p (h c k) -> p h c k", h=2, c=3, k=128)
                    for hh in range(2):
                        for c in range(3):
                            S.add("pe", lambda e, hh=hh, c=c, pT=pT: e.transpose(
                                out=pT[:, hh, c, :], in_=pn[:, hh, c * 128:(c + 1) * 128], identity=self.ident[:]),
                                ["pn"], ["psB"])
                    S.add("dve", lambda e, pT=pT: e.tensor_copy(out=PTs[:], in_=pT), ["psB"], ["PTs"])
                    yield
                    yield
                    for hh in range(2):
                        for c in range(3):
                            S.add("pe", lambda e, hh=hh, c=c, kv=kv, b=b: e.matmul(
                                psW[0:64, hh * 128:(hh + 1) * 128], lhsT=Va_s[:, b + c, kv * 64:(kv + 1) * 64],
                                rhs=PTs[:, hh, c, :], start=(c == 0), stop=(c == 2)), ["PTs"] + atok, ["psX"])
                    S.add("dve", lambda e, h0=h0, OT=OT: e.tensor_copy(
                        out=OT[:, h0:h0 + 2, :], in_=psW[0:64, 0:256].rearrange("p (h k) -> p h k", k=128)),
                        ["psX"], [("OT", b % 2, h0)])
                    yield

        def norm_steps(b, kv):
            OT = OTs[b % 2]
            po = psO[kv]
            h0 = 8 + kv * 4
            S.add("dve", lambda e: e.reciprocal(out=rec[64:65, :], in_=po[64:65, :]), [("psO", kv)], ["rec"])
            yield
            yield
            S.add("pe", lambda e: e.matmul(psB[:, :], lhsT=self.ones32[64:65, :], rhs=rec[64:65, :],
                                           start=True, stop=True), ["rec", "ones32"], ["psB"])
            S.add("dve", lambda e: e.tensor_copy(out=bcs[:], in_=psB[0:64, :]), ["psB"], ["bcs"])
            S.add("dve", lambda e: e.tensor_tensor(
                out=OT[:, h0:h0 + 4, :].rearrange("p h k -> p (h k)"), in0=po[0:64, :], in1=bcs[:], op=ALU.mult),
                [("psO", kv), "bcs"], [("OT", b % 2, h0)])
            yield

        def tail_steps(b):
            OT = OTs[b % 2]
            ottok = [("OT", b % 2, h) for h in (0, 2, 4, 6, 8, 12)]
            S.dma("sp", xt[:], src[b * 128:(b + 1) * 128, :], [], ["xt"], "ldx0")
            yield
            for n in range(2):
                for h in range(16):
                    S.add("pe", lambda e, n=n, h=h: e.matmul(
                        psX[:, :], lhsT=OT[:, h, :], rhs=wo[:, h, n * 512:(n + 1) * 512],
                        start=(h == 0), stop=(h == 15)), ottok + [("wo", h)], ["psX"])
                S.add("dve", lambda e, n=n: e.tensor_tensor(out=xt[:, n * 512:(n + 1) * 512], in0=psX[:, :],
                                                           in1=xt[:, n * 512:(n + 1) * 512], op=ALU.add),
                      ["psX", "xt"], ["xt"])
                yield
            S.dma("sp", self.X[b * 128:(b + 1) * 128, :], xt[:], ["xt"], [], "stx%d" % (b % 2))
            yield

        def chain(*gens):
            for g_ in gens:
                for _ in g_:
                    yield

        for _ in win_steps(0):
            pass
        carry = []
        for b in range(NT):
            qtok = ("qz", b % 2)
            for nb_ in ([0, 1] if b == 0 else [b + 1]):
                if nb_ >= NT:
                    continue
                for kv in range(2):
                    S.dma("sp", qz[nb_ % 2][kv][kv * 64:(kv + 1) * 64, :],
                          self.QTb[kv * 64:(kv + 1) * 64, nb_, :, :].rearrange("p i k -> p (i k)"), [],
                          [("qz", nb_ % 2)], "qb%d" % (nb_ % 2))
            for kv in range(2):
                if kv == 0:
                    for g_ in carry:
                        for _ in g_:
                            pass
                    gens = []
                    if b > 0:
                        gens.append(chain(norm_steps(b - 1, 1), tail_steps(b - 1)))
                    if b + 1 < NT:
                        gens.append(win_steps(b + 1))
                else:
                    gens = [norm_steps(b, 0)] + carry
                po = psO[kv]
                qrhs = qz[b % 2][kv][:, :]

                def qk(c, kv=kv, qrhs=qrhs, qtok=qtok):
                    S.add("pe", lambda e, c=c: e.matmul(
                        psS[c % NS][:], lhsT=KTb_s[:, c * 128:(c + 1) * 128], rhs=qrhs,
                        start=True, stop=True), [qtok] + kvtok, [("psS", c % NS)])

                def ex(c):
                    S.add("act", lambda e, c=c: e.activation(out=pts[c % NP][:], in_=psS[c % NS][:], func=AF.Exp,
                                                             scale=0.125, bias=GRID_SHIFT),
                          [("psS", c % NS)], [("pts", c % NP)])

                def pv(c, kv=kv, po=po):
                    S.add("pe", lambda e, c=c: e.matmul(
                        po[0:65, :], lhsT=Vb_s[:, c, kv, :], rhs=pts[c % NP][:], start=(c == 0), stop=(c == 63)),
                        [("pts", c % NP)] + kvtok, [("psO", kv)])

                for c0 in range(NS - 1):
                    qk(c0)
                for c in range(64):
                    if c + NS - 1 < 64:
                        qk(c + NS - 1)
                    ex(c)
                    pv(c)
                    if c % 2 == 1 and gens:
                        try:
                            next(gens[0])
                        except StopIteration:
                            gens.pop(0)
                carry = gens
        for g_ in carry:
            for _ in g_:
                pass
        for _ in chain(norm_steps(NT - 1, 1), tail_steps(NT - 1)):
            pass
        self.cur_x = ("X", self.X)

    def build(self):
        S = self.S
        nc = self.nc
        S.dma("sp", self.ident[:], self.c_ident, [], ["ident"], "cst0")
        S.add("dve", lambda e: e.memset(self.epsb[:], EPS), [], ["epsb"])
        wph = [p for p in self.phases if p[0] in ("mlp", "sgu", "prep")]
        slot_of = {}
        for k, p in enumerate(wph):
            slot_of[p] = k % 2
        if wph:
            p0 = wph[0]
            self.load_weights(("att", p0[1]) if p0[0] == "prep" else p0, slot_of[p0])
        for p in self.phases:
            S.barrier()
            self.reset_local()
            if p in slot_of:
                k = wph.index(p)
                if k + 1 < len(wph):
                    nx = wph[k + 1]
                    self.load_weights(("att", nx[1]) if nx[0] == "prep" else nx, slot_of[nx])
            kind = p[0]
            if kind == "mlp":
                k_ = self.phases.index(p)
                fuse = (p[2] == 1 and k_ + 1 < len(self.phases) and self.phases[k_ + 1] == ("final",))
                self.phase_mlp(p, slot_of[p], fuse_final=fuse)
                if fuse:
                    self.final_done = True
            elif kind == "sgu":
                self.phase_sgu(p, slot_of[p])
            elif kind == "prep":
                self.phase_prep_att(p, slot_of[p])
                self.last_att_slot = slot_of[p]
            elif kind == "att":
                self.phase_att(p, self.last_att_slot)
            elif kind == "xch":
                self.phase_exchange()
            elif kind == "final":
                if not getattr(self, "final_done", False):
                    self.phase_final()
            elif kind == "copy":
                self.phase_copy_out()
        S.emit()
        return nc


def layer_phases(first, last):
    ph = []
    for l in range(first, last):
        i = l // 2
        if l % 2 == 0:
            ph += [("prep", i), ("xch",), ("att", i)]
        else:
            ph += [("sgu", i)]
        ph += [("mlp", l, 0), ("mlp", l, 1)]
    return ph


def _rope_tables(core):
    half = core % 2
    pos = np.arange(half * T, (half + 1) * T)
    f64 = (10000.0 ** (-np.arange(0, 64, 2, dtype=np.float32) / 64)).astype(np.float32)
    ang = pos.astype(np.float32)[:, None] * f64[None, :]
    r1 = np.stack([np.cos(ang), np.sin(ang)]).astype(np.float32)
    f32_ = (10000.0 ** (-np.arange(0, 32, 2, dtype=np.float32) / 32)).astype(np.float32)
    ar = (pos // 64).astype(np.float32)[:, None] * f32_[None, :]
    ac = (pos % 64).astype(np.float32)[:, None] * f32_[None, :]
    angB = np.concatenate([ar, ac], axis=1)
    rB = np.stack([np.cos(angB), np.sin(angB)]).astype(np.float32)
    to_tiles = lambda a: np.ascontiguousarray(a.reshape(2, NT, 128, 32).transpose(0, 2, 1, 3))
    return to_tiles(r1), to_tiles(rB)


def _masks(core):
    half = core % 2
    qi = np.arange(128)[:, None]
    kj = np.arange(384)[None, :]
    rel = kj - 128 - qi
    band = np.abs(rel) <= 128
    interior = np.where(band, 0.0, -1e30).astype(np.float32)
    first = np.where(band & (kj >= 128), 0.0, -1e30).astype(np.float32)
    last = np.where(band & (kj < 256), 0.0, -1e30).astype(np.float32)
    m0 = first if half == 0 else interior
    m31 = last if half == 1 else interior
    return np.stack([interior, m0, m31]).astype(np.float32).astype(ml_dtypes.bfloat16)


_WNAMES = ["att_norm", "att_w_in", "att_sink", "att_qnorm", "att_knorm", "att_w_out", "sgu_norm", "sgu_w_in",
           "sgu_ln_g", "sgu_ln_b", "sgu_w_s", "sgu_b_s", "sgu_w_out", "mlp_norm", "mlp_w1", "mlp_w2", "final_norm"]

FUSED = True


def _in_maps(inputs, xs, gx=None, used=None):
    ident = np.eye(128, dtype=np.float32).astype(ml_dtypes.bfloat16)
    maps = []
    w = {k: np.ascontiguousarray(np.asarray(inputs[k], dtype=np.float32)) for k in _WNAMES}
    for c in range(NCORES):
        r1, rB = _rope_tables(c)
        m = dict(w)
        m["x"] = xs[c]
        m["c_ident"] = ident
        m["c_rope1"] = r1
        m["c_ropeB"] = rB
        m["c_mask"] = _masks(c)
        if gx is not None:
            m["gx_in"] = gx[c]
        if used is not None:
            m = {k: v for k, v in m.items() if k in used}
        maps.append(m)
    return maps


def _launch(phases, fused, inputs, xs, gx=None):
    b = Builder(phases, fused)
    nc = b.build()
    res = run_bass_kernel_spmd(nc, _in_maps(inputs, xs, gx, set(b._ein)), core_ids=list(range(NCORES)))
    return res.results


def _gather(exs):
    gx = []
    for c in range(NCORES):
        p = c // 2 * 2
        gx.append(np.ascontiguousarray(np.concatenate([exs[p], exs[p + 1]], axis=0)))
    return gx


def kernel(**inputs):
    x = np.ascontiguousarray(np.asarray(inputs["x"], dtype=np.float32))
    xs = [np.ascontiguousarray(x[c // 2, (c % 2) * T:(c % 2 + 1) * T, :]) for c in range(NCORES)]
    cores = list(range(NCORES))
    if FUSED:
        r = _launch(layer_phases(0, 4) + [("final",)], True, inputs, xs)
        outs = [q["out"] for q in r]
    else:
        lp = layer_phases(0, 2)
        r1 = _launch([("prep", 0)], False, inputs, xs)
        gx = _gather([q["ex_out"] for q in r1])
        r2 = _launch(lp[0:1] + lp[2:] + [("prep", 1), ("copy",)], False, inputs, xs, gx)
        xs2 = [q["out"] for q in r2]
        gx2 = _gather([q["ex_out"] for q in r2])
        lp = layer_phases(2, 4)
        r3 = _launch(lp[0:1] + lp[2:] + [("final",)], False, inputs, xs2, gx2)
        outs = [q["out"] for q in r3]
    out = np.empty((4, 8192, D), dtype=np.float32)
    for c in range(NCORES):
        out[c // 2, (c % 2) * T:(c % 2 + 1) * T, :] = outs[c]
    return out
```

```python
import numpy as np
import ml_dtypes
import concourse.bass as bass
import concourse.mybir as mybir
from concourse.bass_utils import run_bass_kernel_spmd

F32 = mybir.dt.float32
BF16 = mybir.dt.bfloat16
AF = mybir.ActivationFunctionType
ALU = mybir.AluOpType
AX = mybir.AxisListType

NCORES = 8
T = 4096
NT = 32
NG = 8
D = 1024
DFF = 4096
EPS = 1e-6
ATT_IN = 1536

ENGS = ("pe", "act", "dve", "pool", "sp")
GRID_SHIFT = -8.0


class Op:
    __slots__ = ("eng", "fn", "deps", "is_dma", "dma_key", "dma_val", "sem_val", "needs_inc", "idx", "dma_inc")

    def __init__(self, eng, fn, is_dma, dma_key):
        self.eng = eng
        self.fn = fn
        self.deps = []
        self.is_dma = is_dma
        self.dma_key = dma_key
        self.dma_val = 0
        self.sem_val = 0
        self.needs_inc = False
        self.idx = 0
        self.dma_inc = 16


class Sched:
    def __init__(self, nc, same_engine_sync=True):
        self.nc = nc
        self.ops = {e: [] for e in ENGS}
        self.last_w = {}
        self.readers = {}
        self.dma_last = {}
        self.dma_cnt = {}
        self.same_engine_sync = same_engine_sync
        self.all_ops = []
        self.barrier_deps = []
        self.barrier_seen = set(ENGS)

    def barrier(self):
        deps = []
        for e in ENGS:
            if self.ops[e]:
                deps.append(self.ops[e][-1])
        for k, op in self.dma_last.items():
            deps.append(op)
        self.barrier_deps = deps
        self.barrier_seen = set()
        self.last_w = {}
        self.readers = {}

    def add(self, eng, fn, reads=(), writes=(), dma_key=None, dma_inc=16):
        is_dma = dma_key is not None
        op = Op(eng, fn, is_dma, dma_key)
        op.dma_inc = dma_inc
        deps = []
        if eng not in self.barrier_seen:
            deps.extend(self.barrier_deps)
            self.barrier_seen.add(eng)
        for t in reads:
            w = self.last_w.get(t)
            if w is not None:
                deps.append(w)
        for t in writes:
            w = self.last_w.get(t)
            if w is not None:
                deps.append(w)
            deps.extend(self.readers.get(t, ()))
        if is_dma:
            p = self.dma_last.get(dma_key)
            if p is not None:
                deps.append(p)
            self.dma_last[dma_key] = op
            self.dma_cnt[dma_key] = self.dma_cnt.get(dma_key, 0) + dma_inc
            op.dma_val = self.dma_cnt[dma_key]
        for t in reads:
            self.readers.setdefault(t, []).append(op)
        for t in writes:
            self.last_w[t] = op
            self.readers[t] = []
        best = {}
        for d in deps:
            if d is op:
                continue
            if d.is_dma:
                k = ("d", d.dma_key)
                v = d.dma_val
            else:
                k = ("e", d.eng)
                v = d.idx
            cur = best.get(k)
            if cur is None or v > cur[0]:
                best[k] = (v, d)
        op.deps = [d for _, d in best.values()]
        op.idx = len(self.ops[eng])
        self.ops[eng].append(op)
        self.all_ops.append(op)
        return op

    def dma(self, eng, out, in_, reads, writes, key, slow=False):
        if slow:
            return self.add(eng, lambda e, o=out, i=in_: e.dma_start(out=o, in_=i, allow_slow_non_contiguous=True),
                            reads, writes, dma_key=key)
        return self.add(eng, lambda e, o=out, i=in_: e.dma_start(out=o, in_=i), reads, writes, dma_key=key)

    def _skip_same(self, d, ename):
        if d.eng != ename:
            return False
        if not self.same_engine_sync:
            return True
        return ename == "pe"

    def emit(self):
        nc = self.nc
        for op in self.all_ops:
            for d in op.deps:
                if d.is_dma or self._skip_same(d, op.eng):
                    continue
                d.needs_inc = True
        for e in ENGS:
            c = 0
            for op in self.ops[e]:
                if op.needs_inc and not op.is_dma:
                    c += 1
                    op.sem_val = c
        eng_sem = {e: nc.alloc_semaphore("prog_" + e) for e in ENGS}
        dma_sem = {k: nc.alloc_semaphore("dq_%d" % i) for i, k in enumerate(self.dma_cnt)}

        def emit_engine(ename, e):
            waited = {}
            for op in self.ops[ename]:
                need = {}
                for d in op.deps:
                    if d.is_dma:
                        s = ("d", d.dma_key)
                        v = d.dma_val
                    else:
                        if self._skip_same(d, ename):
                            continue
                        s = ("e", d.eng)
                        v = d.sem_val
                    if v > need.get(s, 0):
                        need[s] = v
                for s, v in need.items():
                    if waited.get(s, 0) >= v:
                        continue
                    waited[s] = v
                    e.wait_ge(dma_sem[s[1]] if s[0] == "d" else eng_sem[s[1]], v)
                ins = op.fn(e)
                if op.is_dma:
                    ins.then_inc(dma_sem[op.dma_key], op.dma_inc)
                elif op.needs_inc:
                    ins.then_inc(eng_sem[ename], 1)
            if ename == "sp":
                for k, h in dma_sem.items():
                    e.wait_ge(h, self.dma_cnt[k])

        with nc.Block() as block:
            @block.tensor
            def _(e):
                emit_engine("pe", e)

            @block.scalar
            def _(e):
                emit_engine("act", e)

            @block.vector
            def _(e):
                emit_engine("dve", e)

            @block.gpsimd
            def _(e):
                emit_engine("pool", e)

            @block.sync
            def _(e):
                emit_engine("sp", e)


DT_BYTES = {F32: 4, BF16: 2}


class Builder:
    def __init__(self, phases, fused):
        self.nc = nc = bass.Bass("TRN2", target_bir_lowering=False)
        self.S = Sched(nc)
        self.phases = phases
        self.fused = fused
        self.uid = 0
        self.wkey = 0
        self._ein = {}
        self.x_in = self.inp("x")
        self.out = nc.dram_tensor("out", [T, D], F32, kind="ExternalOutput").ap()
        it = lambda n, s, d=F32: nc.dram_tensor(n, s, d).ap()
        self.X = it("Xs", [T, D])
        self.XM = it("XMs", [T, D])
        self.QTa = it("QTa", [128, NT, 4, 128], BF16)
        self.QTb = it("QTb", [128, NT, 4, 128], BF16)
        if fused:
            self.EXk = [it("EXs%d" % k, [128, T], BF16) for k in range(4)]
            self.GXk = [it("GXs%d" % k, [256, T], BF16) for k in range(4)]
        else:
            self.EX = nc.dram_tensor("ex_out", [512, T], BF16, kind="ExternalOutput").ap()
            self.GX = self.inp("gx_in")
        base = (nc.sbuf_base + 63) // 64 * 64
        self.sb_cur = base
        self.sb_lim = nc.sbuf_top
        self.ident = self.salloc("ident", [128, 128], BF16)
        self.ones32 = self.salloc("ones32", [128, 128], F32)
        self.epsb = self.salloc("epsb", [128, 1], F32)
        self.wslot = [self.salloc("wslot0", [128, 32768], BF16), self.salloc("wslot1", [128, 32768], BF16)]
        self.local_base = self.sb_cur
        self.PS = nc.alloc_psum_tensor("ps_all", [128, 8, 512], F32)
        self.ps_cur = 0
        self.cur_x = ("xin", self.x_in)

    IN_SHAPES = {
        "x": ([T, D], F32), "att_norm": ([2, D], F32), "att_w_in": ([2, D, ATT_IN], F32), "att_sink": ([2, 8], F32),
        "att_qnorm": ([2, 64], F32), "att_knorm": ([2, 64], F32), "att_w_out": ([2, D, D], F32),
        "sgu_norm": ([2, D], F32), "sgu_w_in": ([2, D, 2 * D], F32), "sgu_ln_g": ([2, D], F32),
        "sgu_ln_b": ([2, D], F32), "sgu_w_s": ([2, 8, 128, 128], F32), "sgu_b_s": ([2, 8, 128], F32),
        "sgu_w_out": ([2, D, D], F32), "mlp_norm": ([4, D], F32), "mlp_w1": ([4, D, DFF], F32),
        "mlp_w2": ([4, DFF, D], F32), "final_norm": ([D], F32),
        "c_ident": ([128, 128], BF16), "c_rope1": ([2, 128, NT, 32], F32), "c_ropeB": ([2, 128, NT, 32], F32),
        "c_mask": ([3, 128, 384], BF16), "gx_in": ([1024, T], BF16),
    }

    def inp(self, name):
        if name not in self._ein:
            shp, dt_ = self.IN_SHAPES[name]
            self._ein[name] = self.nc.dram_tensor(name, shp, dt_, kind="ExternalInput").ap()
        return self._ein[name]

    def __getattr__(self, name):
        if name in Builder.IN_SHAPES:
            return self.inp(name)
        raise AttributeError(name)

    def ex(self, k):
        return self.EXk[k] if self.fused else self.EX[k * 128:(k + 1) * 128, :]

    def gx(self, r, k):
        if self.fused:
            return self.GXk[k][r * 128:(r + 1) * 128, :]
        return self.GX[r * 512 + k * 128:r * 512 + (k + 1) * 128, :]

    def salloc(self, name, shape, dtype):
        n = 1
        for s in shape[1:]:
            n *= s
        size = (n * DT_BYTES[dtype] + 63) // 64 * 64
        assert self.sb_cur + size <= self.sb_lim, (name, self.sb_cur, size, self.sb_lim)
        self.uid += 1
        t = self.nc.alloc_sbuf_tensor_at("%s_%d" % (name, self.uid), list(shape), dtype, offset=self.sb_cur)
        self.sb_cur += size
        return t

    def reset_local(self):
        self.sb_cur = self.local_base
        self.ps_cur = 0

    def palloc(self, name, shape, dtype=F32):
        n = 1
        for v in shape[1:]:
            n *= v
        nb = (n * DT_BYTES[dtype] + 2047) // 2048
        k = self.ps_cur
        assert k + nb <= 8, (name, k, nb)
        self.ps_cur += nb
        if nb == 1:
            ap = self.PS[:, k, :]
        else:
            ap = self.PS[:, k:k + nb, :].rearrange("p a b -> p (a b)")
        if dtype == BF16:
            ap = ap.bitcast(BF16)
        ap = ap[:, 0:n]
        if len(shape) == 3:
            ap = ap.rearrange("p (a b) -> p a b", b=shape[2])
        return ap

    def wdma(self, dst, src, tok):
        k = "w%d" % (self.wkey % 6)
        self.wkey += 1
        self.S.dma("pool", dst, src, [], [tok], k)

    def wviews(self, kind, slot):
        w = self.wslot[slot]
        if kind == "mlp":
            return (w[:, 0:16384].rearrange("p (c n) -> p c n", n=2048),
                    w[:, 16384:32768].rearrange("p (c n) -> p c n", n=1024))
        if kind == "att":
            return (w[:, 0:12288].rearrange("p (c n) -> p c n", n=1536),
                    w[0:64, 12288:28672].rearrange("p (c n) -> p c n", n=1024))
        if kind == "sgu":
            return (w[:, 0:16384].rearrange("p (c n) -> p c n", n=2048),
                    w[:, 16384:24576].rearrange("p (c n) -> p c n", n=1024),
                    w[:, 24576:25600].rearrange("p (c n) -> p c n", n=128))

    def load_weights(self, ph, slot):
        kind = ph[0]
        if kind == "mlp":
            _, l, hf = ph
            w1, w2 = self.wviews("mlp", slot)
            for kc in range(8):
                self.wdma(w1[:, kc, :], self.mlp_w1[l, kc * 128:(kc + 1) * 128, hf * 2048:(hf + 1) * 2048], ("w1", kc))
            for fc in range(16):
                r0 = (hf * 16 + fc) * 128
                self.wdma(w2[:, fc, :], self.mlp_w2[l, r0:r0 + 128, :], ("w2", fc))
        elif kind == "att":
            _, i = ph
            wi, wo = self.wviews("att", slot)
            for kc in range(8):
                self.wdma(wi[:, kc, :], self.att_w_in[i, kc * 128:(kc + 1) * 128, :], ("wi", kc))
            for h in range(16):
                self.wdma(wo[:, h, :], self.att_w_out[i, h * 64:(h + 1) * 64, :], ("wo", h))
        elif kind == "sgu":
            _, i = ph
            wi, wo, ws = self.wviews("sgu", slot)
            for kc in range(8):
                self.wdma(wi[:, kc, :], self.sgu_w_in[i, kc * 128:(kc + 1) * 128, :], ("wi", kc))
            for kc in range(8):
                self.wdma(wo[:, kc, :], self.sgu_w_out[i, kc * 128:(kc + 1) * 128, :], ("wo", kc))
            self.wdma(ws, self.sgu_w_s[i].rearrange("g p q -> p g q"), ("ws",))

    def prep_alloc(self):
        P = {}
        P["gt"] = self.salloc("gt", [128, D], F32)
        P["junk"] = self.salloc("junk", [128, D], BF16)
        P["ss"] = self.salloc("ss", [128, 16], F32)
        P["hb"] = [self.salloc("hb", [128, D], BF16) for _ in range(2)]
        P["hT"] = [self.salloc("hT", [128, 8, 512], BF16) for _ in range(2)]
        P["pT"] = self.palloc("pT", [128, 8, 128], BF16)
        return P

    def load_gain(self, P, gvec):
        self.S.dma("sp", P["gt"][:], gvec.partition_broadcast(128), [], ["gt"], "cst0")

    def prep_ew(self, P, xg, xtok, g):
        S = self.S
        s = g % 2
        ss = P["ss"][:, s * 8:(s + 1) * 8]
        for j in range(4):
            S.add("act", lambda e, j=j: e.activation(out=P["junk"][:], in_=xg[:, j, :], func=AF.Square,
                                                     accum_out=ss[:, j:j + 1]),
                  [xtok], ["junk", ("ss", j)])
        S.add("act", lambda e: e.activation(out=ss[:, 4:8], in_=ss[:, 0:4], func=AF.Ln, scale=1.0 / D,
                                            bias=self.epsb[:, 0:1]),
              [("ss", j) for j in range(4)], [("rs", s)])
        S.add("act", lambda e: e.activation(out=ss[:, 4:8], in_=ss[:, 4:8], func=AF.Exp, scale=-0.5),
              [("rs", s)], [("rs", s)])

    def prep_tr(self, P, xg, xtok, g, evac="act", part=None):
        S = self.S
        s = g % 2
        hT = P["hT"][s]

        def hb_op(j):
            hb = P["hb"][j % 2]
            S.add("dve", lambda e, j=j, hb=hb: e.scalar_tensor_tensor(out=hb[:], in0=xg[:, j, :],
                                                                      scalar=P["ss"][:, s * 8 + 4 + j:s * 8 + 5 + j], in1=P["gt"][:],
                                                                      op0=ALU.mult, op1=ALU.mult),
                  [xtok, ("rs", s), "gt"], [("hb", j % 2)])

        if part in (None, "early"):
            hb_op(0)
            hb_op(1)
        if part == "early":
            return
        for j in range(4):
            hb = P["hb"][j % 2]
            if j >= 2:
                hb_op(j)
            for c in range(8):
                S.add("pe", lambda e, c=c, hb=hb: e.transpose(out=P["pT"][:, c, :], in_=hb[:, c * 128:(c + 1) * 128],
                                                              identity=self.ident[:]),
                      [("hb", j % 2)], ["pT"])
            if evac == "act":
                S.add("act", lambda e, j=j: e.copy(out=hT[:, :, j * 128:(j + 1) * 128], in_=P["pT"][:]),
                      ["pT"], [("hT", s)])
            else:
                S.add("dve", lambda e, j=j: e.tensor_copy(out=hT[:, :, j * 128:(j + 1) * 128], in_=P["pT"][:]),
                      ["pT"], [("hT", s)])

    def xview(self, ap, g):
        return ap[g * 512:(g + 1) * 512, :].rearrange("(j p) d -> p j d", p=128)

    def phase_mlp(self, ph, slot, fuse_final=False):
        _, l, hf = ph
        S = self.S
        w1, w2 = self.wviews("mlp", slot)
        P = self.prep_alloc()
        bufs = [self.salloc("xbuf", [128, 4, D], F32) for _ in range(2)]
        aT = self.salloc("aT", [128, 16, 512], BF16)
        rb = [self.salloc("rb", [128, 512], BF16) for _ in range(2)]
        pm = [self.palloc("pm", [128, 512]) for _ in range(2)]
        po = [self.palloc("po", [128, 2, 512]) for _ in range(2)]
        srcname, src = self.cur_x
        self.load_gain(P, self.mlp_norm[l])
        if hf == 0:
            dst = self.XM
        else:
            dst = self.X
        if fuse_final:
            dst = self.out
            gtf = self.wslot[1 - slot][:, 0:2048].bitcast(F32)
            fs = self.salloc("fs", [128, 8], F32)
            S.dma("sp", gtf, self.final_norm.partition_broadcast(128), [], ["gtf"], "cst1")

        def xbuf(g):
            return (bufs[g % 2], ("xb", g % 2)) if hf == 0 else (bufs[0], ("xb", 0))

        def load_x(g):
            b, tok = xbuf(g)
            S.dma("sp", b[:], self.xview(src, g), [], [tok], "ldx%d" % (g % 2))

        def load_m(g):
            S.dma("sp", bufs[1][:], self.xview(self.XM, g), [], [("xb", 1)], "ldm")

        load_x(0)
        if hf == 1:
            load_m(0)
        b0, t0 = xbuf(0)
        self.prep_ew(P, b0, t0, 0)
        self.prep_tr(P, b0, t0, 0)
        for g in range(NG):
            hT = P["hT"][g % 2]
            for fc in range(16):
                ps = pm[fc % 2]
                for kc in range(8):
                    S.add("pe", lambda e, ps=ps, kc=kc, fc=fc, hT=hT: e.matmul(
                        ps[:], lhsT=w1[:, kc, fc * 128:(fc + 1) * 128], rhs=hT[:, kc, :],
                        start=(kc == 0), stop=(kc == 7)),
                        [("w1", kc), ("hT", g % 2)], [("pm", fc % 2)])
                r = rb[fc % 2]
                S.add("act", lambda e, ps=ps, r=r: e.activation(out=r[:], in_=ps[:], func=AF.Relu),
                      [("pm", fc % 2)], [("rb", fc % 2)])
                S.add("dve", lambda e, r=r, fc=fc: e.tensor_tensor(out=aT[:, fc, :], in0=r[:], in1=r[:], op=ALU.mult),
                      [("rb", fc % 2)], [("aT", fc)])
            if g + 1 < NG:
                load_x(g + 1)
                nb, nt = xbuf(g + 1)
                self.prep_ew(P, nb, nt, g + 1)
                self.prep_tr(P, nb, nt, g + 1, part="early")
            acc, acctok = (xbuf(g) if hf == 0 else (bufs[1], ("xb", 1)))
            for j in range(4):
                if j == 2 and g + 1 < NG:
                    nb, nt = xbuf(g + 1)
                    self.prep_tr(P, nb, nt, g + 1, part="late")
                pso = po[j % 2]
                for n in range(2):
                    for fc in range(16):
                        S.add("pe", lambda e, pso=pso, n=n, fc=fc, j=j: e.matmul(
                            pso[:, n, :], lhsT=aT[:, fc, j * 128:(j + 1) * 128], rhs=w2[:, fc, n * 512:(n + 1) * 512],
                            start=(fc == 0), stop=(fc == 15)),
                            [("aT", fc), ("w2", fc)], [("po", j % 2)])
                S.add("dve", lambda e, pso=pso, j=j, acc=acc: e.tensor_tensor(
                    out=acc[:, j, :], in0=pso.rearrange("p a b -> p (a b)"), in1=acc[:, j, :], op=ALU.add),
                    [("po", j % 2), acctok], [acctok])
            if fuse_final:
                for j in range(4):
                    S.add("act", lambda e, j=j, acc=acc: e.activation(out=P["junk"][:], in_=acc[:, j, :], func=AF.Square,
                                                                      accum_out=fs[:, j:j + 1]),
                          [acctok], ["junk", ("fs", j)])
                S.add("act", lambda e: e.activation(out=fs[:, 4:8], in_=fs[:, 0:4], func=AF.Ln, scale=1.0 / D,
                                                    bias=self.epsb[:, 0:1]), [("fs", j) for j in range(4)], ["frs"])
                S.add("act", lambda e: e.activation(out=fs[:, 4:8], in_=fs[:, 4:8], func=AF.Exp, scale=-0.5),
                      ["frs"], ["frs"])
                for j in range(4):
                    S.add("dve", lambda e, j=j, acc=acc: e.scalar_tensor_tensor(
                        out=acc[:, j, :], in0=acc[:, j, :], scalar=fs[:, 4 + j:5 + j], in1=gtf,
                        op0=ALU.mult, op1=ALU.mult), [acctok, "frs", "gtf"], [acctok])
            S.dma("sp", self.xview(dst, g), acc[:], [acctok], [], "stx%d" % (g % 2))
            if hf == 1 and g + 1 < NG:
                load_m(g + 1)
        if hf == 1:
            self.cur_x = ("X", self.X)

    def phase_final(self):
        S = self.S
        P = self.prep_alloc()
        bufs = [self.salloc("xbuf", [128, 4, D], F32) for _ in range(2)]
        _, src = self.cur_x
        self.load_gain(P, self.final_norm)
        for g in range(NG):
            b = bufs[g % 2]
            tok = ("xb", g % 2)
            S.dma("sp", b[:], self.xview(src, g), [], [tok], "ldx%d" % (g % 2))
            self.prep_ew(P, b, tok, g)
            for j in range(4):
                S.add("dve", lambda e, j=j, b=b, g=g: e.scalar_tensor_tensor(
                    out=b[:, j, :], in0=b[:, j, :], scalar=P["ss"][:, (g % 2) * 8 + 4 + j:(g % 2) * 8 + 5 + j], in1=P["gt"][:],
                    op0=ALU.mult, op1=ALU.mult), [tok, ("rs", g % 2), "gt"], [tok])
            S.dma("sp", self.xview(self.out, g), b[:], [tok], [], "stx%d" % (g % 2))

    def phase_copy_out(self):
        _, src = self.cur_x
        for g in range(NG):
            self.S.dma("sp", self.out[g * 512:(g + 1) * 512, :], src[g * 512:(g + 1) * 512, :], [], [], "cp%d" % (g % 4))

    def phase_sgu(self, ph, slot):
        _, i = ph
        S = self.S
        wi, wo, ws = self.wviews("sgu", slot)
        wsl = self.wslot[slot]
        P = self.prep_alloc()
        xb = self.salloc("xbuf", [128, 4, D], F32)
        lng = wsl[:, 25600:27648].bitcast(F32)
        lnb = wsl[:, 27648:29696].bitcast(F32)
        bst = self.salloc("bst", [128, 8], F32)
        wsT = self.salloc("wsT", [128, 8, 128], BF16)
        tmp = self.salloc("tmp", [128, D], F32)
        sets = []
        for k in range(2):
            sets.append(dict(
                u=self.salloc("u", [128, D], BF16), v=self.salloc("v", [128, D], F32),
                vnb=self.salloc("vnb", [128, D], BF16), yb=self.salloc("yb", [128, D], BF16),
                yT=self.salloc("yT", [128, 8, 128], BF16), st=self.salloc("st", [128, 4, 6], F32),
                mv=self.salloc("mv", [128, 4], F32)))
        pzk = self.ps_cur
        pz = [self.palloc("pz", [128, 512]) for _ in range(4)]
        pz2 = [self.PS[:, pzk + 2 * q:pzk + 2 * q + 2, :].rearrange("p a b -> p (a b)") for q in range(2)]
        pq = self.palloc("pq", [128, 2, 512])
        _, src = self.cur_x
        self.load_gain(P, self.sgu_norm[i])
        S.dma("sp", lng, self.sgu_ln_g[i].partition_broadcast(128), [], ["lng"], "cst1")
        S.dma("sp", lnb, self.sgu_ln_b[i].partition_broadcast(128), [], ["lnb"], "cst2")
        S.dma("sp", bst[:], self.sgu_b_s[i].rearrange("g p -> p g"), [], ["bst"], "cst3", slow=True)
        for gi in range(8):
            S.add("pe", lambda e, gi=gi: e.transpose(out=P["pT"][:, gi, :], in_=ws[:, gi, :], identity=self.ident[:]),
                  [("ws",)], ["pT"])
        S.add("act", lambda e: e.copy(out=wsT[:], in_=P["pT"][:]), ["pT"], ["wsT"])
        tok = ("xb", 0)
        pzc = [0]

        def tile_steps(g, j, hT):
            B = sets[j % 2]
            k_ = j % 2
            u, v, vnb, yb, yT, st, mv = B["u"], B["v"], B["vnb"], B["yb"], B["yT"], B["st"], B["mv"]
            for n in range(4):
                pb = pzc[0] % 4
                pzc[0] += 1
                for kc in range(8):
                    S.add("pe", lambda e, n=n, kc=kc, pb=pb: e.matmul(
                        pz[pb][:, :], lhsT=hT[:, kc, j * 128:(j + 1) * 128], rhs=wi[:, kc, n * 512:(n + 1) * 512],
                        start=(kc == 0), stop=(kc == 7)), [("hT", g % 2), ("wi", kc)], [("pz", pb)])
                if n % 2 == 1:
                    dst = u[:, :] if n == 1 else v[:, :]
                    src2 = pz2[(pb - 1) // 2]
                    S.add("act", lambda e, dst=dst, src2=src2: e.activation(out=dst, in_=src2, func=AF.Gelu_apprx_tanh),
                          [("pz", pb - 1), ("pz", pb)], [("u", k_) if n == 1 else ("v", k_)])
            yield
            for k in range(2):
                S.add("dve", lambda e, k=k: e.bn_stats(out=st[:, k, :], in_=v[:, k * 512:(k + 1) * 512]),
                      [("v", k_)], [("st", k_, k)])
            S.add("dve", lambda e: e.bn_aggr(out=mv[:, 0:2], in_=st[:, 0:2, :]), [("st", k_, 0), ("st", k_, 1)], [("mv", k_)])
            yield
            S.add("act", lambda e: e.activation(out=mv[:, 2:3], in_=mv[:, 1:2], func=AF.Ln, bias=self.epsb[:, 0:1]),
                  [("mv", k_)], [("mv2", k_)])
            S.add("act", lambda e: e.activation(out=mv[:, 2:3], in_=mv[:, 2:3], func=AF.Exp, scale=-0.5),
                  [("mv2", k_)], [("mv2", k_)])
            yield
            S.add("dve", lambda e: e.tensor_scalar(out=tmp[:], in0=v[:], scalar1=mv[:, 0:1], scalar2=mv[:, 2:3],
                                                  op0=ALU.subtract, op1=ALU.mult), [("v", k_), ("mv", k_), ("mv2", k_)], ["tmp"])
            S.add("dve", lambda e: e.tensor_tensor(out=tmp[:], in0=tmp[:], in1=lng, op=ALU.mult),
                  ["tmp", "lng"], ["tmp"])
            S.add("dve", lambda e: e.tensor_tensor(out=vnb[:], in0=tmp[:], in1=lnb, op=ALU.add),
                  ["tmp", "lnb"], [("vnb", k_)])
            yield
            for gi in range(8):
                S.add("pe", lambda e, gi=gi: e.matmul(
                    pq[:, gi // 4, (gi % 4) * 128:(gi % 4 + 1) * 128], lhsT=wsT[:, gi, :],
                    rhs=vnb[:, gi * 128:(gi + 1) * 128], start=True, stop=True), ["wsT", ("vnb", k_)], ["pq"])
            S.add("dve", lambda e: e.tensor_tensor(
                out=tmp[:].rearrange("p (g d) -> p g d", d=128),
                in0=pq.rearrange("p a (g d) -> p (a g) d", d=128),
                in1=bst[:, 0:8].unsqueeze(2).to_broadcast([128, 8, 128]), op=ALU.add), ["pq", "bst"], ["tmp"])
            S.add("dve", lambda e: e.tensor_tensor(out=yb[:], in0=tmp[:], in1=u[:], op=ALU.mult),
                  ["tmp", ("u", k_)], [("yb", k_)])
            yield
            for c in range(8):
                S.add("pe", lambda e, c=c: e.transpose(out=P["pT"][:, c, :], in_=yb[:, c * 128:(c + 1) * 128],
                                                       identity=self.ident[:]), [("yb", k_)], ["pT"])
            S.add("act", lambda e: e.copy(out=yT[:], in_=P["pT"][:]), ["pT"], [("yT", k_)])
            yield
            for n in range(2):
                for kc in range(8):
                    S.add("pe", lambda e, n=n, kc=kc: e.matmul(
                        pq[:, n, :], lhsT=yT[:, kc, :], rhs=wo[:, kc, n * 512:(n + 1) * 512],
                        start=(kc == 0), stop=(kc == 7)), [("yT", k_), ("wo", kc)], ["pq"])
            S.add("dve", lambda e: e.tensor_tensor(
                out=xb[:, j, :], in0=pq.rearrange("p a b -> p (a b)"), in1=xb[:, j, :], op=ALU.add),
                ["pq", tok], [tok])
            yield

        for g in range(NG):
            S.dma("sp", xb[:], self.xview(src, g), [], [tok], "ldx0")
            self.prep_ew(P, xb, tok, g)
            self.prep_tr(P, xb, tok, g)
            hT = P["hT"][g % 2]
            todo = [tile_steps(g, j, hT) for j in range(4)]
            active = [todo.pop(0), todo.pop(0)]
            while active:
                for g_ in list(active):
                    try:
                        next(g_)
                    except StopIteration:
                        active.remove(g_)
                        if todo:
                            active.append(todo.pop(0))
            S.dma("sp", self.xview(self.X, g), xb[:], [tok], [], "stx%d" % (g % 2))
        self.cur_x = ("X", self.X)

    def phase_prep_att(self, ph, slot):
        _, i = ph
        S = self.S
        wi, wo = self.wviews("att", slot)
        wsl = self.wslot[slot]
        P = self.prep_alloc()
        xb = self.salloc("xbuf", [128, 4, D], F32)
        rope1 = self.salloc("rope1", [128, 2, 4, 32], F32)
        ropeB = self.salloc("ropeB", [128, 2, 4, 32], F32)
        gB = self.salloc("gB", [128, 10, 64], F32)
        pjs = [self.salloc("pjs", [128, ATT_IN], F32), wsl[:, 28672:31744].bitcast(F32)]
        t1 = self.salloc("t1", [128, 640], F32)
        t2 = self.salloc("t2", [128, 320], F32)
        xn = self.salloc("xn", [128, 640], F32)
        p1 = wsl[:, 31744:32384].bitcast(F32)
        p2 = self.salloc("p2", [128, 320], F32)
        sbs = [self.salloc("sb", [128, 32], F32) for _ in range(2)]
        rots = [self.salloc("rot", [128, 10, 128], BF16) for _ in range(2)]
        qst = self.salloc("qst", [128, 4, 10, 128], BF16)
        vst = self.salloc("vst", [128, 4, 2, 128], BF16)
        pj = [self.palloc("pj", [128, 512]) for _ in range(4)]
        pq = self.palloc("pqT", [128, 10, 128], BF16)
        _, src = self.cur_x
        self.load_gain(P, self.att_norm[i])
        for h in range(10):
            gv = self.att_qnorm[i] if h < 8 else self.att_knorm[i]
            S.dma("sp", gB[:, h, :], gv.partition_broadcast(128), [], [("gB", h)], "cst%d" % (5 + h % 2))
        gBtok = [("gB", h) for h in range(10)]
        ropetok = ["rope1", "rope1x", "ropeB", "ropeBx"]
        KTb, KTa = self.ex(0), self.ex(1)
        Vb = self.ex(2).rearrange("r (t d) -> (r t) d", d=128)
        Va = self.ex(3).rearrange("r (t d) -> (r t) d", d=128)
        tok = ("xb", 0)
        pjc = [0]

        def tile_steps(g, j, hT):
            k_ = j % 2
            pj_s, rot, sb = pjs[k_], rots[k_], sbs[k_]
            for n in range(3):
                pb = pjc[0] % 4
                pjc[0] += 1
                for kc in range(8):
                    S.add("pe", lambda e, n=n, kc=kc, pb=pb: e.matmul(
                        pj[pb][:, :], lhsT=hT[:, kc, j * 128:(j + 1) * 128], rhs=wi[:, kc, n * 512:(n + 1) * 512],
                        start=(kc == 0), stop=(kc == 7)), [("hT", g % 2), ("wi", kc)], [("pj", pb)])
                S.add("act", lambda e, n=n, pb=pb: e.copy(out=pj_s[:, n * 512:(n + 1) * 512], in_=pj[pb][:, :]),
                      [("pj", pb)], [("pjs", k_)])
            S.add("act", lambda e: e.copy(out=vst[:, j, 0, :], in_=pj_s[:, 640:768]), [("pjs", k_)], [("vst", j)])
            S.add("act", lambda e: e.copy(out=vst[:, j, 1, :], in_=pj_s[:, 1408:1536]), [("pjs", k_)], [("vst", j)])
            yield
            cosA = rope1[:, 0, j, :]
            sinA = rope1[:, 1, j, :]
            segs = []
            for kv in range(2):
                segs.append((pj_s[:, kv * 256:(kv + 1) * 256].rearrange("p (i d) -> p i d", d=64),
                             rot[:, 0:4, kv * 64:(kv + 1) * 64], 4))
            segs.append((pj_s[:, 512:640].rearrange("p (i d) -> p i d", d=64),
                         rot[:, 4, :].rearrange("p (i d) -> p i d", d=64), 2))
            rtA = [("pjs", k_)] + ropetok
            plan = []
            off = 0
            for si, (xi, xo, nh) in enumerate(segs):
                cb = cosA.unsqueeze(1).to_broadcast([128, nh, 32])
                sbb = sinA.unsqueeze(1).to_broadcast([128, nh, 32])
                x1, x2 = xi[:, :, 0:32], xi[:, :, 32:64]
                a1 = p1[:, off:off + nh * 32].rearrange("p (i d) -> p i d", d=32)
                a2 = p2[:, off:off + nh * 32].rearrange("p (i d) -> p i d", d=32)
                off += nh * 32
                T1, T2 = ("p1", si), ("p2", si)
                plan.append([
                    (lambda e, a1=a1, x1=x1, cb=cb: e.tensor_tensor(out=a1, in0=x1, in1=cb, op=ALU.mult), rtA, [T1]),
                    (lambda e, a2=a2, x2=x2, sbb=sbb: e.tensor_tensor(out=a2, in0=x2, in1=sbb, op=ALU.mult), rtA, [T2]),
                    (lambda e, xo=xo, a1=a1, a2=a2: e.tensor_tensor(out=xo[:, :, 0:32], in0=a1, in1=a2, op=ALU.subtract),
                     [T1, T2], [("rotA", k_)]),
                    (lambda e, a1=a1, x2=x2, cb=cb: e.tensor_tensor(out=a1, in0=x2, in1=cb, op=ALU.mult), rtA, [T1]),
                    (lambda e, a2=a2, x1=x1, sbb=sbb: e.tensor_tensor(out=a2, in0=x1, in1=sbb, op=ALU.mult), rtA, [T2]),
                    (lambda e, xo=xo, a1=a1, a2=a2: e.tensor_tensor(out=xo[:, :, 32:64], in0=a1, in1=a2, op=ALU.add),
                     [T1, T2], [("rotA", k_)]),
                ])
            for step in range(6):
                for seg_ops in plan:
                    f, r_, w_ = seg_ops[step]
                    S.add("pool", f, r_, w_)
            xbv = pj_s[:, 768:1408]
            S.add("dve", lambda e: e.tensor_tensor(out=t1[:], in0=xbv, in1=xbv, op=ALU.mult), [("pjs", k_)], ["t1"])
            S.add("dve", lambda e: e.tensor_reduce(out=sb[:, 0:10], in_=t1[:].rearrange("p (h d) -> p h d", d=64),
                                                  axis=AX.X, op=ALU.add), ["t1"], [("sb", k_)])
            yield
            S.add("act", lambda e: e.activation(out=sb[:, 16:26], in_=sb[:, 0:10], func=AF.Ln, scale=1.0 / 64,
                                                bias=self.epsb[:, 0:1]), [("sb", k_)], [("sb2", k_)])
            S.add("act", lambda e: e.activation(out=sb[:, 16:26], in_=sb[:, 16:26], func=AF.Exp, scale=-0.5),
                  [("sb2", k_)], [("sb2", k_)])
            yield
            S.add("dve", lambda e: e.tensor_tensor(
                out=t1[:].rearrange("p (h d) -> p h d", d=64), in0=xbv.rearrange("p (h d) -> p h d", d=64),
                in1=sb[:, 16:26].unsqueeze(2).to_broadcast([128, 10, 64]), op=ALU.mult), [("pjs", k_), ("sb2", k_), "t1"], ["t1"])
            S.add("dve", lambda e: e.tensor_tensor(out=xn[:], in0=t1[:], in1=gB[:].rearrange("p h d -> p (h d)"),
                                                  op=ALU.mult), ["t1"] + gBtok, ["xn"])
            cosB = ropeB[:, 0, j, :].rearrange("p (a d) -> p a d", d=16)
            sinB = ropeB[:, 1, j, :].rearrange("p (a d) -> p a d", d=16)
            segs = []
            for kv in range(2):
                segs.append((xn[:, kv * 256:(kv + 1) * 256].rearrange("p (i a c d) -> p i a c d", a=2, c=2, d=16),
                             rot[:, 5:9, kv * 64:(kv + 1) * 64].rearrange("p i (a c d) -> p i a c d", a=2, c=2, d=16), 4))
            segs.append((xn[:, 512:640].rearrange("p (i a c d) -> p i a c d", a=2, c=2, d=16),
                         rot[:, 9, :].rearrange("p (i a c d) -> p i a c d", a=2, c=2, d=16), 2))
            plan = []
            off = 0
            for si, (xi, xo, nh) in enumerate(segs):
                cb = cosB.unsqueeze(1).to_broadcast([128, nh, 2, 16])
                sbb = sinB.unsqueeze(1).to_broadcast([128, nh, 2, 16])
                x1, x2 = xi[:, :, :, 0, :], xi[:, :, :, 1, :]
                a1 = t1[:, off:off + nh * 32].rearrange("p (i a d) -> p i a d", a=2, d=16)
                a2 = t2[:, off:off + nh * 32].rearrange("p (i a d) -> p i a d", a=2, d=16)
                off += nh * 32
                rt = ["xn"] + ropetok
                T1, T2 = ("t1", si), ("t2", si)
                first = ["t1"] if True else []
                plan.append([
                    (lambda e, a1=a1, x1=x1, cb=cb: e.tensor_tensor(out=a1, in0=x1, in1=cb, op=ALU.mult), rt + ["t1"], [T1]),
                    (lambda e, a2=a2, x2=x2, sbb=sbb: e.tensor_tensor(out=a2, in0=x2, in1=sbb, op=ALU.mult), rt, [T2]),
                    (lambda e, xo=xo, a1=a1, a2=a2: e.tensor_tensor(out=xo[:, :, :, 0, :], in0=a1, in1=a2, op=ALU.subtract),
                     [T1, T2], [("rotB", k_)]),
                    (lambda e, a1=a1, x2=x2, cb=cb: e.tensor_tensor(out=a1, in0=x2, in1=cb, op=ALU.mult), rt, [T1]),
                    (lambda e, a2=a2, x1=x1, sbb=sbb: e.tensor_tensor(out=a2, in0=x1, in1=sbb, op=ALU.mult), rt, [T2]),
                    (lambda e, xo=xo, a1=a1, a2=a2: e.tensor_tensor(out=xo[:, :, :, 1, :], in0=a1, in1=a2, op=ALU.add),
                     [T1, T2], [("rotB", k_)]),
                ])
            for step in range(6):
                for seg_ops in plan:
                    f, r_, w_ = seg_ops[step]
                    S.add("dve", f, r_, w_)
            S.add("dve", lambda e: e.memset(sb[:, 30:31], 0.0), [("t1", 0), ("t1", 1), ("t1", 2), ("t2", 0)], ["t1"])
            yield
            for c in range(10):
                S.add("pe", lambda e, c=c: e.transpose(out=pq[:, c, :], in_=rot[:, c, :], identity=self.ident[:]),
                      [("rotA", k_), ("rotB", k_)], ["pqT"])
            S.add("act", lambda e: e.copy(out=qst[:, j, :, :], in_=pq), ["pqT"], [("qst", j)])
            yield

        def group_prologue(g):
            S.dma("sp", xb[:], self.xview(src, g), [], [tok], "ldx%d" % (g % 2))
            yield
            self.prep_ew(P, xb, tok, g)
            yield
            yield
            self.prep_tr(P, xb, tok, g)
            yield

        for _ in group_prologue(0):
            pass
        for g in range(NG):
            for k in range(2):
                S.dma("sp", rope1[:, k, :, :], self.c_rope1[k][:, g * 4:(g + 1) * 4, :], [],
                      ["rope1"] if k else ["rope1x"], "cst%d" % (1 + k))
                S.dma("sp", ropeB[:, k, :, :], self.c_ropeB[k][:, g * 4:(g + 1) * 4, :], [],
                      ["ropeB"] if k else ["ropeBx"], "cst%d" % (3 + k))
            hT = P["hT"][g % 2]
            todo = [tile_steps(g, j, hT) for j in range(4)]
            active = [todo.pop(0), todo.pop(0)]
            side = group_prologue(g + 1) if g + 1 < NG else None
            while active:
                for g_ in list(active):
                    try:
                        next(g_)
                    except StopIteration:
                        active.remove(g_)
                        if todo:
                            active.append(todo.pop(0))
                if side is not None:
                    try:
                        next(side)
                    except StopIteration:
                        side = None
            if side is not None:
                for _ in side:
                    pass
            qtok = [("qst", j) for j in range(4)]
            t0 = g * 4
            S.dma("sp", self.QTa[:, t0:t0 + 4, :, :], qst[:, :, 0:4, :], qtok, [], "sq0")
            S.dma("sp", self.QTb[:, t0:t0 + 4, :, :], qst[:, :, 5:9, :], qtok, [], "sq1")
            S.dma("sp", KTa[:, g * 512:(g + 1) * 512].rearrange("p (j k) -> p j k", k=128), qst[:, :, 4, :], qtok, [], "sq2")
            S.dma("sp", KTb[:, g * 512:(g + 1) * 512].rearrange("p (j k) -> p j k", k=128), qst[:, :, 9, :], qtok, [], "sq3")
            vtok = [("vst", j) for j in range(4)]
            S.dma("sp", Va[g * 512:(g + 1) * 512, :].rearrange("(j p) d -> p j d", p=128), vst[:, :, 0, :], vtok, [], "sq4")
            S.dma("sp", Vb[g * 512:(g + 1) * 512, :].rearrange("(j p) d -> p j d", p=128), vst[:, :, 1, :], vtok, [], "sq5")

    def phase_exchange(self):
        for k in range(4):
            self.S.add("pool", lambda e, k=k: e.collective_compute(
                "AllGather", ALU.bypass, replica_groups=[[0, 1], [2, 3], [4, 5], [6, 7]],
                ins=[self.EXk[k].opt()], outs=[self.GXk[k].opt()]), [], [], dma_key="cc%d" % k, dma_inc=1)

    def phase_att(self, ph, slot):
        _, i = ph
        S = self.S
        wi, wo = self.wviews("att", slot)
        wsl = self.wslot[slot]
        KTb_s = self.salloc("KTb", [128, 8192], BF16)
        Vb_s = self.salloc("Vb", [128, 64, 2, 65], BF16)
        KTa_s = self.salloc("KTa", [128, 34, 128], BF16)
        Va_s = self.salloc("Va", [128, 34, 128], BF16)
        msk = self.salloc("msk", [128, 3, 384], BF16)
        sink = self.salloc("sink", [128, 8], F32)
        gq = self.salloc("gq", [128, 2, 64], F32)
        nsh = self.salloc("nsh", [128, 4], F32)
        qa = self.salloc("qa", [128, 4, 128], BF16)
        qz = [[self.salloc("qz", [128, 512], BF16), wsl[:, 30720:31232]],
              [wsl[:, 31232:31744], wsl[:, 31744:32256]]]
        NP = 6
        pts = [self.salloc("pts", [128, 512], BF16) for _ in range(NP)]
        OTs = [self.salloc("OT", [64, 16, 128], BF16),
               wsl[0:64, 28672:30720].rearrange("p (h k) -> p h k", k=128)]
        sm = self.salloc("sm", [128, 2, 384], F32)
        pe_ = self.salloc("pexp", [128, 2, 384], BF16)
        pn = self.salloc("pn", [128, 2, 384], BF16)
        PTs = self.salloc("PTs", [128, 2, 3, 128], BF16)
        st = self.salloc("stt", [128, 16], F32)
        xt = self.salloc("xt", [128, D], F32)
        rec = wsl[:, 12288:13312].bitcast(F32)
        bcs = self.salloc("bcs", [64, 512], F32)
        NS = 4
        psS = [self.palloc("psS", [128, 512]) for _ in range(NS)]
        psO = [self.palloc("psO", [128, 512]) for _ in range(2)]
        psB = self.palloc("psB", [128, 512])
        psX = self.palloc("psX", [128, 512])
        psW = psX
        psA = psX
        _, src = self.cur_x
        KTa_own = self.ex(1)
        Va_own = self.ex(3).rearrange("r (t d) -> (r t) d", d=128)
        for r in range(2):
            S.dma("sp", KTb_s[:, r * T:(r + 1) * T], self.gx(r, 0), [], [("KTb", r)], "ca%d" % r)
            vsrc = self.gx(r, 2).rearrange("r (t d) -> (r t) d", d=128)
            for q in range(4):
                c0 = r * 32 + q * 8
                for kv in range(2):
                    S.dma("sp", Vb_s[:, c0:c0 + 8, kv, 0:64],
                          vsrc[q * 1024:(q + 1) * 1024, kv * 64:(kv + 1) * 64].rearrange("(c p) d -> p c d", p=128),
                          [], [("Vb", r, q, kv)], "cb%d" % (q * 2 + kv))
        S.add("pool", lambda e: e.memset(Vb_s[:, :, :, 64:65], 1.0), [], ["Vb1"])
        S.add("pool", lambda e: e.memset(self.ones32[:], 1.0), [], ["ones32"])
        for par in range(2):
            for kv in range(2):
                S.add("pool", lambda e, par=par, kv=kv: e.memset(qz[par][kv][:, :], 0.0), [], [("qz", par)])
        kvtok = [("KTb", 0), ("KTb", 1), "Vb1"] + [("Vb", r, q, kv) for r in range(2) for q in range(4) for kv in range(2)]
        S.dma("sp", KTa_s[:, 1:33, :], KTa_own.rearrange("p (c k) -> p c k", k=128), [], ["KTa0"], "ce0")
        S.dma("sp", KTa_s[:, 0, :], self.gx(0, 1)[:, T - 128:T], [], ["KTa1"], "ce1")
        S.dma("sp", KTa_s[:, 33, :], self.gx(1, 1)[:, 0:128], [], ["KTa2"], "ce2")
        S.dma("sp", Va_s[:, 1:33, :], Va_own.rearrange("(c p) d -> p c d", p=128), [], ["Va0"], "ce3")
        va0 = self.gx(0, 3).rearrange("r (t d) -> (r t) d", d=128)
        va1 = self.gx(1, 3).rearrange("r (t d) -> (r t) d", d=128)
        S.dma("sp", Va_s[:, 0, :], va0[T - 128:T, :], [], ["Va1"], "ce4")
        S.dma("sp", Va_s[:, 33, :], va1[0:128, :], [], ["Va2"], "ce5")
        atok = ["KTa0", "KTa1", "KTa2", "Va0", "Va1", "Va2"]
        S.dma("sp", msk[:], self.c_mask.rearrange("m p k -> p m k"), [], ["msk"], "cd0")
        S.dma("sp", sink[:], self.att_sink[i].partition_broadcast(128), [], ["sink"], "cd1")
        S.dma("sp", gq[:, 0, :], self.att_qnorm[i].partition_broadcast(128), [], ["gq0"], "cd2")
        S.dma("sp", gq[:, 1, :], self.att_knorm[i].partition_broadcast(128), [], ["gq1"], "cd3")
        S.add("dve", lambda e: e.tensor_reduce(out=nsh[:, 0:2], in_=gq[:], axis=AX.X, op=ALU.max,
                                              apply_absolute_value=True), ["gq0", "gq1"], ["nsh0"])
        S.add("dve", lambda e: e.tensor_tensor(out=nsh[:, 2:3], in0=nsh[:, 0:1], in1=nsh[:, 1:2], op=ALU.mult),
              ["nsh0"], ["nsh1"])
        S.add("dve", lambda e: e.tensor_scalar(out=nsh[:, 3:4], in0=nsh[:, 2:3], scalar1=-8.0, scalar2=None,
                                              op0=ALU.mult), ["nsh1"], ["nsh"])

        def win_steps(b):
            OT = OTs[b % 2]
            S.dma("sp", qa[:], self.QTa[:, b, :, :], [], ["qa"], "qa")
            mi = 1 if b == 0 else (2 if b == NT - 1 else 0)
            for kv in range(2):
                for hp in range(2):
                    h0 = kv * 4 + hp * 2
                    for hh in range(2):
                        ih = hp * 2 + hh
                        S.add("pe", lambda e, kv=kv, ih=ih, b=b: e.matmul(
                            psA[:, 0:384], lhsT=qa[kv * 64:(kv + 1) * 64, ih, :],
                            rhs=KTa_s[kv * 64:(kv + 1) * 64, b:b + 3, :].rearrange("p c k -> p (c k)"),
                            start=True, stop=True), ["qa"] + atok, ["psX"])
                        S.add("dve", lambda e, hh=hh, mi=mi: e.scalar_tensor_tensor(
                            out=sm[:, hh, :], in0=psA[:, 0:384], scalar=0.125, in1=msk[:, mi, :],
                            op0=ALU.mult, op1=ALU.add), ["psX", "msk"], ["sm"])
                    yield
                    S.add("dve", lambda e: e.tensor_reduce(out=st[:, 0:2], in_=sm[:], axis=AX.X, op=ALU.max),
                          ["sm"], ["st0"])
                    S.add("dve", lambda e, h0=h0: e.tensor_tensor(out=st[:, 2:4], in0=st[:, 0:2], in1=sink[:, h0:h0 + 2],
                                                                 op=ALU.max), ["st0", "sink"], ["st1"])
                    S.add("dve", lambda e: e.tensor_scalar(out=st[:, 4:6], in0=st[:, 2:4], scalar1=-1.0, scalar2=None,
                                                          op0=ALU.mult), ["st1"], ["st2"])
                    S.add("dve", lambda e, h0=h0: e.tensor_tensor(out=st[:, 6:8], in0=sink[:, h0:h0 + 2], in1=st[:, 2:4],
                                                                 op=ALU.subtract), ["st1", "sink"], ["st3"])
                    yield
                    yield
                    for hh in range(2):
                        S.add("act", lambda e, hh=hh: e.activation(out=pe_[:, hh, :], in_=sm[:, hh, :], func=AF.Exp,
                                                                   bias=st[:, 4 + hh:5 + hh], accum_out=st[:, 8 + hh:9 + hh]),
                              ["sm", "st2"], ["pexp", ("den", hh)])
                    S.add("act", lambda e: e.activation(out=st[:, 10:12], in_=st[:, 6:8], func=AF.Exp), ["st3"], ["st4"])
                    yield
                    S.add("dve", lambda e: e.tensor_tensor(out=st[:, 12:14], in0=st[:, 8:10], in1=st[:, 10:12], op=ALU.add),
                          [("den", 0), ("den", 1), "st4"], ["st5"])
                    S.add("dve", lambda e: e.reciprocal(out=st[:, 14:16], in_=st[:, 12:14]), ["st5"], ["st6"])
                    for hh in range(2):
                        S.add("dve", lambda e, hh=hh: e.tensor_scalar(out=pn[:, hh, :], in0=pe_[:, hh, :],
                                                                     scalar1=st[:, 14 + hh:15 + hh], scalar2=None,
                                                                     op0=ALU.mult), ["pexp", "st6"], ["pn"])
                    yield
                    yield
                    pT = psB[:, 0:384].bitcast(BF16).rearrange("p (h c k) -> p h c k", h=2, c=3, k=128)
                    for hh in range(2):
                        for c in range(3):
                            S.add("pe", lambda e, hh=hh, c=c, pT=pT: e.transpose(
                                out=pT[:, hh, c, :], in_=pn[:, hh, c * 128:(c + 1) * 128], identity=self.ident[:]),
                                ["pn"], ["psB"])
                    S.add("dve", lambda e, pT=pT: e.tensor_copy(out=PTs[:], in_=pT), ["psB"], ["PTs"])
                    yield
                    yield
                    for hh in range(2):
                        for c in range(3):
                            S.add("pe", lambda e, hh=hh, c=c, kv=kv, b=b: e.matmul(
                                psW[0:64, hh * 128:(hh + 1) * 128], lhsT=Va_s[:, b + c, kv * 64:(kv + 1) * 64],
                                rhs=PTs[:, hh, c, :], start=(c == 0), stop=(c == 2)), ["PTs"] + atok, ["psX"])
                    S.add("dve", lambda e, h0=h0, OT=OT: e.tensor_copy(
                        out=OT[:, h0:h0 + 2, :], in_=psW[0:64, 0:256].rearrange("p (h k) -> p h k", k=128)),
                        ["psX"], [("OT", b % 2, h0)])
                    yield

        def norm_steps(b, kv):
            OT = OTs[b % 2]
            po = psO[kv]
            h0 = 8 + kv * 4
            S.add("dve", lambda e: e.reciprocal(out=rec[64:65, :], in_=po[64:65, :]), [("psO", kv)], ["rec"])
            yield
            yield
            S.add("pe", lambda e: e.matmul(psB[:, :], lhsT=self.ones32[64:65, :], rhs=rec[64:65, :],
                                           start=True, stop=True), ["rec", "ones32"], ["psB"])
            S.add("dve", lambda e: e.tensor_copy(out=bcs[:], in_=psB[0:64, :]), ["psB"], ["bcs"])
            S.add("dve", lambda e: e.tensor_tensor(
                out=OT[:, h0:h0 + 4, :].rearrange("p h k -> p (h k)"), in0=po[0:64, :], in1=bcs[:], op=ALU.mult),
                [("psO", kv), "bcs"], [("OT", b % 2, h0)])
            yield

        def tail_steps(b):
            OT = OTs[b % 2]
            ottok = [("OT", b % 2, h) for h in (0, 2, 4, 6, 8, 12)]
            S.dma("sp", xt[:], src[b * 128:(b + 1) * 128, :], [], ["xt"], "ldx0")
            yield
            for n in range(2):
                for h in range(16):
                    S.add("pe", lambda e, n=n, h=h: e.matmul(
                        psX[:, :], lhsT=OT[:, h, :], rhs=wo[:, h, n * 512:(n + 1) * 512],
                        start=(h == 0), stop=(h == 15)), ottok + [("wo", h)], ["psX"])
                S.add("dve", lambda e, n=n: e.tensor_tensor(out=xt[:, n * 512:(n + 1) * 512], in0=psX[:, :],
                                                           in1=xt[:, n * 512:(n + 1) * 512], op=ALU.add),
                      ["psX", "xt"], ["xt"])
                yield
            S.dma("sp", self.X[b * 128:(b + 1) * 128, :], xt[:], ["xt"], [], "stx%d" % (b % 2))
            yield

        def chain(*gens):
            for g_ in gens:
                for _ in g_:
                    yield

        for _ in win_steps(0):
            pass
        carry = []
        for b in range(NT):
            qtok = ("qz", b % 2)
            for nb_ in ([0, 1] if b == 0 else [b + 1]):
                if nb_ >= NT:
                    continue
                for kv in range(2):
                    S.dma("sp", qz[nb_ % 2][kv][kv * 64:(kv + 1) * 64, :],
                          self.QTb[kv * 64:(kv + 1) * 64, nb_, :, :].rearrange("p i k -> p (i k)"), [],
                          [("qz", nb_ % 2)], "qb%d" % (nb_ % 2))
            for kv in range(2):
                if kv == 0:
                    for g_ in carry:
                        for _ in g_:
                            pass
                    gens = []
                    if b > 0:
                        gens.append(chain(norm_steps(b - 1, 1), tail_steps(b - 1)))
                    if b + 1 < NT:
                        gens.append(win_steps(b + 1))
                else:
                    gens = [norm_steps(b, 0)] + carry
                po = psO[kv]
                qrhs = qz[b % 2][kv][:, :]

                def qk(c, kv=kv, qrhs=qrhs, qtok=qtok):
                    S.add("pe", lambda e, c=c: e.matmul(
                        psS[c % NS][:], lhsT=KTb_s[:, c * 128:(c + 1) * 128], rhs=qrhs,
                        start=True, stop=True), [qtok] + kvtok, [("psS", c % NS)])

                def ex(c):
                    S.add("act", lambda e, c=c: e.activation(out=pts[c % NP][:], in_=psS[c % NS][:], func=AF.Exp,
                                                             scale=0.125, bias=GRID_SHIFT),
                          [("psS", c % NS)], [("pts", c % NP)])

                def pv(c, kv=kv, po=po):
                    S.add("pe", lambda e, c=c: e.matmul(
                        po[0:65, :], lhsT=Vb_s[:, c, kv, :], rhs=pts[c % NP][:], start=(c == 0), stop=(c == 63)),
                        [("pts", c % NP)] + kvtok, [("psO", kv)])

                for c0 in range(NS - 1):
                    qk(c0)
                for c in range(64):
                    if c + NS - 1 < 64:
                        qk(c + NS - 1)
                    ex(c)
                    pv(c)
                    if c % 2 == 1 and gens:
                        try:
                            next(gens[0])
                        except StopIteration:
                            gens.pop(0)
                carry = gens
        for g_ in carry:
            for _ in g_:
                pass
        for _ in chain(norm_steps(NT - 1, 1), tail_steps(NT - 1)):
            pass
        self.cur_x = ("X", self.X)

    def build(self):
        S = self.S
        nc = self.nc
        S.dma("sp", self.ident[:], self.c_ident, [], ["ident"], "cst0")
        S.add("dve", lambda e: e.memset(self.epsb[:], EPS), [], ["epsb"])
        wph = [p for p in self.phases if p[0] in ("mlp", "sgu", "prep")]
        slot_of = {}
        for k, p in enumerate(wph):
            slot_of[p] = k % 2
        if wph:
            p0 = wph[0]
            self.load_weights(("att", p0[1]) if p0[0] == "prep" else p0, slot_of[p0])
        for p in self.phases:
            S.barrier()
            self.reset_local()
            if p in slot_of:
                k = wph.index(p)
                if k + 1 < len(wph):
                    nx = wph[k + 1]
                    self.load_weights(("att", nx[1]) if nx[0] == "prep" else nx, slot_of[nx])
            kind = p[0]
            if kind == "mlp":
                k_ = self.phases.index(p)
                fuse = (p[2] == 1 and k_ + 1 < len(self.phases) and self.phases[k_ + 1] == ("final",))
                self.phase_mlp(p, slot_of[p], fuse_final=fuse)
                if fuse:
                    self.final_done = True
            elif kind == "sgu":
                self.phase_sgu(p, slot_of[p])
            elif kind == "prep":
                self.phase_prep_att(p, slot_of[p])
                self.last_att_slot = slot_of[p]
            elif kind == "att":
                self.phase_att(p, self.last_att_slot)
            elif kind == "xch":
                self.phase_exchange()
            elif kind == "final":
                if not getattr(self, "final_done", False):
                    self.phase_final()
            elif kind == "copy":
                self.phase_copy_out()
        S.emit()
        return nc


def layer_phases(first, last):
    ph = []
    for l in range(first, last):
        i = l // 2
        if l % 2 == 0:
            ph += [("prep", i), ("xch",), ("att", i)]
        else:
            ph += [("sgu", i)]
        ph += [("mlp", l, 0), ("mlp", l, 1)]
    return ph


def _rope_tables(core):
    half = core % 2
    pos = np.arange(half * T, (half + 1) * T)
    f64 = (10000.0 ** (-np.arange(0, 64, 2, dtype=np.float32) / 64)).astype(np.float32)
    ang = pos.astype(np.float32)[:, None] * f64[None, :]
    r1 = np.stack([np.cos(ang), np.sin(ang)]).astype(np.float32)
    f32_ = (10000.0 ** (-np.arange(0, 32, 2, dtype=np.float32) / 32)).astype(np.float32)
    ar = (pos // 64).astype(np.float32)[:, None] * f32_[None, :]
    ac = (pos % 64).astype(np.float32)[:, None] * f32_[None, :]
    angB = np.concatenate([ar, ac], axis=1)
    rB = np.stack([np.cos(angB), np.sin(angB)]).astype(np.float32)
    to_tiles = lambda a: np.ascontiguousarray(a.reshape(2, NT, 128, 32).transpose(0, 2, 1, 3))
    return to_tiles(r1), to_tiles(rB)


def _masks(core):
    half = core % 2
    qi = np.arange(128)[:, None]
    kj = np.arange(384)[None, :]
    rel = kj - 128 - qi
    band = np.abs(rel) <= 128
    interior = np.where(band, 0.0, -1e30).astype(np.float32)
    first = np.where(band & (kj >= 128), 0.0, -1e30).astype(np.float32)
    last = np.where(band & (kj < 256), 0.0, -1e30).astype(np.float32)
    m0 = first if half == 0 else interior
    m31 = last if half == 1 else interior
    return np.stack([interior, m0, m31]).astype(np.float32).astype(ml_dtypes.bfloat16)


_WNAMES = ["att_norm", "att_w_in", "att_sink", "att_qnorm", "att_knorm", "att_w_out", "sgu_norm", "sgu_w_in",
           "sgu_ln_g", "sgu_ln_b", "sgu_w_s", "sgu_b_s", "sgu_w_out", "mlp_norm", "mlp_w1", "mlp_w2", "final_norm"]

FUSED = True


def _in_maps(inputs, xs, gx=None, used=None):
    ident = np.eye(128, dtype=np.float32).astype(ml_dtypes.bfloat16)
    maps = []
    w = {k: np.ascontiguousarray(np.asarray(inputs[k], dtype=np.float32)) for k in _WNAMES}
    for c in range(NCORES):
        r1, rB = _rope_tables(c)
        m = dict(w)
        m["x"] = xs[c]
        m["c_ident"] = ident
        m["c_rope1"] = r1
        m["c_ropeB"] = rB
        m["c_mask"] = _masks(c)
        if gx is not None:
            m["gx_in"] = gx[c]
        if used is not None:
            m = {k: v for k, v in m.items() if k in used}
        maps.append(m)
    return maps


def _launch(phases, fused, inputs, xs, gx=None):
    b = Builder(phases, fused)
    nc = b.build()
    res = run_bass_kernel_spmd(nc, _in_maps(inputs, xs, gx, set(b._ein)), core_ids=list(range(NCORES)))
    return res.results


def _gather(exs):
    gx = []
    for c in range(NCORES):
        p = c // 2 * 2
        gx.append(np.ascontiguousarray(np.concatenate([exs[p], exs[p + 1]], axis=0)))
    return gx


def kernel(**inputs):
    x = np.ascontiguousarray(np.asarray(inputs["x"], dtype=np.float32))
    xs = [np.ascontiguousarray(x[c // 2, (c % 2) * T:(c % 2 + 1) * T, :]) for c in range(NCORES)]
    cores = list(range(NCORES))
    if FUSED:
        r = _launch(layer_phases(0, 4) + [("final",)], True, inputs, xs)
        outs = [q["out"] for q in r]
    else:
        lp = layer_phases(0, 2)
        r1 = _launch([("prep", 0)], False, inputs, xs)
        gx = _gather([q["ex_out"] for q in r1])
        r2 = _launch(lp[0:1] + lp[2:] + [("prep", 1), ("copy",)], False, inputs, xs, gx)
        xs2 = [q["out"] for q in r2]
        gx2 = _gather([q["ex_out"] for q in r2])
        lp = layer_phases(2, 4)
        r3 = _launch(lp[0:1] + lp[2:] + [("final",)], False, inputs, xs2, gx2)
        outs = [q["out"] for q in r3]
    out = np.empty((4, 8192, D), dtype=np.float32)
    for c in range(NCORES):
        out[c // 2, (c % 2) * T:(c % 2 + 1) * T, :] = outs[c]
    return out
```
